# Optimizing a Trainium2 kernel written in Bass

```python
import math
import jax, jax.numpy as jnp
from jax import lax
import numpy as np

D_MODEL = 1024
BATCH = 16
SEQ = 2048
DEPTH = 1

D_MIX = D_MODEL
CONV_CH = D_MIX // 2
CONV_GROUPS = 8
CONV_K = 31
FOX_HEADS = 8
FOX_HEAD_DIM = 64
FOX_W = FOX_HEADS * FOX_HEAD_DIM
Q_BLOCK = 128
MEM_LEN = 256
MEM_HEADS = 4
MEM_HEAD_DIM = D_MODEL // MEM_HEADS
D_FF = ((8 * D_MODEL // 3 + 255) // 256) * 256
EPS = 1e-6

OFF_U = 0
OFF_G = OFF_U + CONV_CH
OFF_Q = OFF_G + CONV_CH
OFF_K = OFF_Q + FOX_W
OFF_V = OFF_K + FOX_W
OFF_F = OFF_V + FOX_W
D_IN = OFF_F + FOX_HEADS

kernel_name = "hybrid_conformer_fox_memory_block"


def rmsnorm(x, g):
    xf = x.astype(jnp.float32)
    y = xf * lax.rsqrt(jnp.mean(xf * xf, axis=-1, keepdims=True) + EPS)
    return (y * g.astype(jnp.float32)).astype(x.dtype)


def layernorm(x, g, b):
    xf = x.astype(jnp.float32)
    mu = jnp.mean(xf, axis=-1, keepdims=True)
    xc = xf - mu
    y = xc * lax.rsqrt(jnp.mean(xc * xc, axis=-1, keepdims=True) + EPS)
    return (y * g.astype(jnp.float32) + b.astype(jnp.float32)).astype(x.dtype)


def conformer_conv(u, gate, conv_w, conv_b, ln_g, ln_b):
    a = u * jax.nn.sigmoid(gate)
    y = lax.conv_general_dilated(
        a, conv_w[:, None, :].astype(a.dtype),
        window_strides=(1,), padding=[(CONV_K - 1, 0)],
        dimension_numbers=("NWC", "WIO", "NWC"),
        feature_group_count=CONV_CH) + conv_b.astype(a.dtype)
    return jax.nn.silu(layernorm(y, ln_g, ln_b))


def forgetting_attention(q, k, v, logf):
    b, s, h, dh = q.shape
    scale = 1.0 / math.sqrt(dh)
    qh = jnp.transpose(q, (0, 2, 1, 3))
    kh = jnp.transpose(k, (0, 2, 1, 3))
    vh = jnp.transpose(v, (0, 2, 1, 3))
    c = jnp.transpose(jnp.cumsum(logf, axis=1), (0, 2, 1))
    outs = []
    for i in range(s // Q_BLOCK):
        q0, end = i * Q_BLOCK, (i + 1) * Q_BLOCK
        logits = jnp.einsum("bhqd,bhkd->bhqk", qh[:, :, q0:end], kh[:, :, :end],
                            preferred_element_type=jnp.float32) * scale
        logits = logits + (c[:, :, q0:end, None] - c[:, :, None, :end])
        causal = jnp.arange(end)[None, :] <= (q0 + jnp.arange(Q_BLOCK))[:, None]
        logits = jnp.where(causal[None, None], logits, -jnp.inf)
        p = jax.nn.softmax(logits, axis=-1)
        outs.append(jnp.einsum("bhqk,bhkd->bhqd", p.astype(vh.dtype), vh[:, :, :end]))
    o = jnp.concatenate(outs, axis=2)
    return jnp.transpose(o, (0, 2, 1, 3)).reshape(b, s, h * dh)


def memory_cross_attention(hx, mem_n, w_mq, w_mkv, w_mo):
    b, s, _ = hx.shape
    m = mem_n.shape[1]
    q = (hx @ w_mq).reshape(b, s, MEM_HEADS, MEM_HEAD_DIM)
    kv = mem_n @ w_mkv
    k = kv[..., :D_MODEL].reshape(b, m, MEM_HEADS, MEM_HEAD_DIM)
    v = kv[..., D_MODEL:].reshape(b, m, MEM_HEADS, MEM_HEAD_DIM)
    logits = jnp.einsum("bshd,bmhd->bhsm", q, k,
                        preferred_element_type=jnp.float32) / math.sqrt(MEM_HEAD_DIM)
    p = jax.nn.softmax(logits, axis=-1)
    o = jnp.einsum("bhsm,bmhd->bshd", p.astype(v.dtype), v).reshape(b, s, D_MODEL)
    return o @ w_mo


def setup_inputs(seed: int = 0) -> dict:
    key = jax.random.key(seed)
    ks = jax.random.split(key, 24)
    f32 = jnp.float32

    def nrm(k, shape, fan_in):
        return jax.random.normal(k, shape, f32) * (fan_in ** -0.5)

    def gain(k, shape):
        return 1.0 + 0.02 * jax.random.normal(k, shape, f32)

    def small(k, shape, s=0.02):
        return s * jax.random.normal(k, shape, f32)

    return {
        "x": jax.random.normal(ks[0], (BATCH, SEQ, D_MODEL), f32),
        "mem": jax.random.normal(ks[1], (BATCH, MEM_LEN, D_MODEL), f32),
        "g_mix": gain(ks[2], (DEPTH, D_MODEL)),
        "w_in": nrm(ks[3], (DEPTH, D_MODEL, D_IN), D_MODEL),
        "b_f": 2.0 + small(ks[4], (DEPTH, FOX_HEADS), 0.5),
        "conv_w": nrm(ks[5], (DEPTH, CONV_K, CONV_CH), CONV_K),
        "conv_b": small(ks[6], (DEPTH, CONV_CH)),
        "ln_g": gain(ks[7], (DEPTH, CONV_CH)),
        "ln_b": small(ks[8], (DEPTH, CONV_CH)),
        "w_out": nrm(ks[9], (DEPTH, D_MIX, D_MODEL), D_MIX),
        "g_x": gain(ks[10], (DEPTH, D_MODEL)),
        "g_mem": gain(ks[11], (D_MODEL,)),
        "w_mq": nrm(ks[12], (DEPTH, D_MODEL, D_MODEL), D_MODEL),
        "w_mkv": nrm(ks[13], (DEPTH, D_MODEL, 2 * D_MODEL), D_MODEL),
        "w_mo": nrm(ks[14], (DEPTH, D_MODEL, D_MODEL), D_MODEL),
        "g_ffn": gain(ks[15], (DEPTH, D_MODEL)),
        "w_gu": nrm(ks[16], (DEPTH, D_MODEL, 2 * D_FF), D_MODEL),
        "w_down": nrm(ks[17], (DEPTH, D_FF, D_MODEL), D_FF),
        "g_final": gain(ks[18], (D_MODEL,)),
    }


def reference(x, mem, g_mix, w_in, b_f, conv_w, conv_b, ln_g, ln_b, w_out,
              g_x, g_mem, w_mq, w_mkv, w_mo, g_ffn, w_gu, w_down, g_final):
    b, s, _ = x.shape
    mem_n = rmsnorm(mem, g_mem)
    for l in range(DEPTH):
        h = rmsnorm(x, g_mix[l])
        z = h @ w_in[l]
        conv_out = conformer_conv(z[..., OFF_U:OFF_G], z[..., OFF_G:OFF_Q],
                                  conv_w[l], conv_b[l], ln_g[l], ln_b[l])
        q = z[..., OFF_Q:OFF_K].reshape(b, s, FOX_HEADS, FOX_HEAD_DIM)
        k = z[..., OFF_K:OFF_V].reshape(b, s, FOX_HEADS, FOX_HEAD_DIM)
        v = z[..., OFF_V:OFF_F].reshape(b, s, FOX_HEADS, FOX_HEAD_DIM)
        logf = jax.nn.log_sigmoid((z[..., OFF_F:] + b_f[l]).astype(jnp.float32))
        att_out = forgetting_attention(q, k, v, logf)
        x = x + jnp.concatenate([conv_out, att_out], axis=-1) @ w_out[l]
        x = x + memory_cross_attention(rmsnorm(x, g_x[l]), mem_n, w_mq[l], w_mkv[l], w_mo[l])
        gu = rmsnorm(x, g_ffn[l]) @ w_gu[l]
        x = x + (jax.nn.silu(gu[..., :D_FF]) * gu[..., D_FF:]) @ w_down[l]
    return rmsnorm(x, g_final)
```

```python
from contextlib import ExitStack

import numpy as np
import ml_dtypes

import concourse.bass as bass
import concourse.mybir as mybir
from concourse.bass_utils import run_bass_kernel_spmd

F32 = mybir.dt.float32
BF16 = mybir.dt.bfloat16
AF = mybir.ActivationFunctionType
ALU = mybir.AluOpType

NCORES = 8
NB = 2
S = 2048
D = 1024
KC = 8
DIN = 2568
OFF_U, OFF_G, OFF_Q, OFF_K, OFF_V, OFF_F = 0, 512, 1024, 1536, 2048, 2560
DFF = 2816
FJ = 22
MEM = 256
NT = S // 128
TG = 256
TPG = TG // 128
TGC = 512
CK = 31
EPS = 1e-6
SBUF_BASE = 16512
SBUF_LIMIT = 229312


class Buf:
    __slots__ = ("name", "writers", "readers")

    def __init__(self, name):
        self.name = name
        self.writers = []
        self.readers = []


class Op:
    __slots__ = ("eng", "fn", "deps", "needs_inc", "ticket", "lane", "lane_count", "is_dma")

    def __init__(self, eng, fn):
        self.eng = eng
        self.fn = fn
        self.deps = []
        self.needs_inc = False
        self.ticket = None
        self.lane = None
        self.lane_count = None
        self.is_dma = False


ENGS = ("pe", "act", "dve", "pool", "sp")


class Sched:
    def __init__(self, nc, stack):
        self.nc = nc
        self.stack = stack
        self.streams = {e: [] for e in ENGS}
        self.lane_counts = {}
        self.lane_last = {}
        self.lane_sems = {}
        self.eng_sems = {}
        self.pending = {e: [] for e in ENGS}
        self.nbuf = 0

    def buf(self, name=None):
        self.nbuf += 1
        return Buf(name or "b%d" % self.nbuf)

    def bufs(self, n, name="b"):
        return [self.buf("%s%d" % (name, i)) for i in range(n)]

    def _add_dep(self, op, dep):
        if dep is op:
            return
        if dep.is_dma:
            op.deps.append((dep, self.lane_counts[dep.lane]))
        else:
            if dep.eng == "pe" and op.eng == "pe" and not op.is_dma:
                return
            op.deps.append((dep, None))

    def barrier(self, exclude_lanes=()):
        lasts = []
        for e in ENGS:
            for o in reversed(self.streams[e]):
                if not o.is_dma:
                    lasts.append(o)
                    break
        lasts.extend(o for ln, o in self.lane_last.items() if ln not in exclude_lanes)
        for e in ENGS:
            self.pending[e] = list(lasts)

    def op(self, eng, fn, reads=(), writes=(), lane=None):
        o = Op(eng, fn)
        if lane is not None:
            o.is_dma = True
            o.lane = lane
        if self.pending[eng]:
            for d in self.pending[eng]:
                if d.is_dma or d.eng != eng:
                    self._add_dep(o, d)
            self.pending[eng] = []
        for b in reads:
            for w in b.writers:
                self._add_dep(o, w)
        for b in writes:
            same_gen = (
                o.is_dma and b.writers and not b.readers
                and all(w.is_dma and w.lane == lane for w in b.writers)
            )
            if same_gen:
                b.writers.append(o)
                continue
            for r in b.readers:
                self._add_dep(o, r)
            for w in b.writers:
                self._add_dep(o, w)
            b.writers = [o]
            b.readers = []
        for b in reads:
            if o not in b.readers:
                b.readers.append(o)
        if o.is_dma:
            self.lane_counts[lane] = self.lane_counts.get(lane, 0) + 1
            o.lane_count = self.lane_counts[lane]
            self.lane_last[lane] = o
        self.streams[eng].append(o)
        return o

    def finalize(self):
        nc = self.nc
        for e in ENGS:
            for o in self.streams[e]:
                for d, _ in o.deps:
                    if not d.is_dma:
                        d.needs_inc = True
        for e in ENGS:
            t = 0
            for o in self.streams[e]:
                if o.needs_inc and not o.is_dma:
                    t += 1
                    o.ticket = t
        for e in ENGS:
            self.eng_sems[e] = self.stack.enter_context(nc.semaphore("s_" + e))
        for ln in self.lane_counts:
            self.lane_sems[ln] = self.stack.enter_context(nc.semaphore("l_%s" % (ln,)))

    def replay(self, e, engine, final_lanes=()):
        known = {}
        for o in self.streams[e]:
            need = {}
            for d, lc in o.deps:
                if d.is_dma:
                    key = ("lane", d.lane)
                    val = 16 * lc
                else:
                    key = ("eng", d.eng)
                    val = d.ticket
                if known.get(key, 0) >= val:
                    continue
                if need.get(key, 0) < val:
                    need[key] = val
            for key, val in need.items():
                sem = self.lane_sems[key[1]] if key[0] == "lane" else self.eng_sems[key[1]]
                engine.wait_ge(sem, val)
                known[key] = val
            ins = o.fn(engine)
            if o.is_dma:
                ins.then_inc(self.lane_sems[o.lane], 16)
            elif o.needs_inc:
                ins.then_inc(self.eng_sems[e], 1)
        for ln in final_lanes:
            engine.wait_ge(self.lane_sems[ln], 16 * self.lane_counts[ln])

    def run(self, final_lanes=()):
        nc = self.nc
        self.finalize()
        print("ops", {e: len(v) for e, v in self.streams.items()},
              "tickets", {e: max([o.ticket or 0 for o in v] + [0]) for e, v in self.streams.items()},
              "lanes", len(self.lane_counts), max(self.lane_counts.values()))
        with nc.Block() as block:
            @block.tensor
            def _(eng):
                self.replay("pe", eng)

            @block.scalar
            def _(eng):
                self.replay("act", eng)

            @block.vector
            def _(eng):
                self.replay("dve", eng)

            @block.gpsimd
            def _(eng):
                self.replay("pool", eng)

            @block.sync
            def _(eng):
                self.replay("sp", eng, final_lanes=final_lanes)


class Arena:
    def __init__(self, nc, base=0):
        self.nc = nc
        self.off = base
        self.n = 0

    def alloc(self, name, shape, dt):
        nbytes = int(np.prod(shape[1:])) * (4 if dt == F32 else 2)
        nbytes = (nbytes + 31) // 32 * 32
        self.n += 1
        t = self.nc.alloc_sbuf_tensor_at("%s_%d_%d" % (name, self.off, self.n), list(shape), dt, offset=self.off)
        self.off += nbytes
        assert self.off <= SBUF_LIMIT, (name, self.off)
        return t


def build_program(debug=False, stop=None, nb_a=NB, ng_a=S // TG, nt_b=NB * NT, ng_c=NB * S // TGC):
    nc = bass.Bass("TRN2", target_bir_lowering=False)

    def din(name, shape, dt=F32):
        return nc.dram_tensor(name, list(shape), dt, kind="ExternalInput").ap()

    x = din("x", [NB, S, D])
    mem = din("mem", [NB, MEM, D])
    w_in = din("w_in", [D, DIN])
    w_out = din("w_out", [D, D])
    w_mq = din("w_mq", [D, D])
    w_mkv = din("w_mkv", [D, 2 * D])
    w_mo = din("w_mo", [D, D])
    w_gu = din("w_gu", [D, 2 * DFF])
    w_down = din("w_down", [DFF, D])
    g_mix = din("g_mix", [D])
    g_x = din("g_x", [D])
    g_mem = din("g_mem", [D])
    g_ffn = din("g_ffn", [D])
    g_final = din("g_final", [D])
    b_f = din("b_f", [8])
    convw = din("convw", [128, 4, CK])
    convb = din("convb", [128, 4])
    lng = din("lng", [128, 4])
    lnb = din("lnb", [128, 4])
    c_identb = din("c_identb", [128, 128], BF16)
    c_identf = din("c_identf", [128, 128])
    c_utri = din("c_utri", [128, 128])
    c_onesf = din("c_onesf", [128, 128])
    c_mask = din("c_mask", [128, 128], BF16)
    c_onesb = din("c_onesb", [1, 8 * S], BF16)
    y = nc.dram_tensor("y", [NB, S, D], F32, kind="ExternalOutput").ap()
    skind = "ExternalOutput" if debug else "Internal"
    x1s = nc.dram_tensor("x1s", [NB, S, D], F32, kind=skind).ap()
    x2s = nc.dram_tensor("x2s", [NB, S, D], F32, kind=skind).ap()

    with ExitStack() as st:
        S_ = Sched(nc, st)
        op = S_.op
        nb = S_.buf

        def ps(name, shape, dt=F32):
            return st.enter_context(nc.psum_tensor(name, list(shape), dt))

        AR = Arena(nc, SBUF_BASE)
        identb = AR.alloc("identb", [128, 128], BF16)
        identf = AR.alloc("identf", [128, 128], F32)
        utri = AR.alloc("utri", [128, 128], F32)
        onesf = AR.alloc("onesf", [128, 128], F32)
        cmask = AR.alloc("cmask", [128, 128], BF16)
        inv512 = AR.alloc("inv512", [128, 128], BF16)
        oneb = AR.alloc("oneb", [128, 8], BF16)
        epst = AR.alloc("epst", [128, 1], F32)
        onec = AR.alloc("onec", [128, 1], F32)
        ss = [AR.alloc("ss", [128, 1], F32) for _ in range(4)]
        lnt = [AR.alloc("lnt", [128, 1], F32) for _ in range(4)]
        rstd = [AR.alloc("rstd", [128, 1], F32) for _ in range(4)]
        junk = AR.alloc("junk", [128, D], BF16)
        xin = [AR.alloc("xin", [128, D], F32) for _ in range(2)]
        xres = [AR.alloc("xres", [128, D], F32) for _ in range(2)]
        hb = [AR.alloc("hb", [128, D], BF16) for _ in range(2)]
        PBASE = AR.off

        B_const = nb("const")
        B_ss = S_.bufs(4, "ss")
        B_lnt = S_.bufs(4, "lnt")
        B_rstd = S_.bufs(4, "rstd")
        B_junk = nb("junk")
        B_xin = S_.bufs(2, "xin")
        B_xres = S_.bufs(2, "xres")
        B_hb = S_.bufs(2, "hb")

        p_tr = ps("p_tr", [128, 8, 128], BF16)
        p_z = [ps("p_z%d" % i, [128, 512]) for i in range(2)]
        p_cv = ps("p_cv", [128, 2, 256])
        p_st = ps("p_st", [128, 2, 256])
        p_s = [ps("p_s%d" % i, [128, 2, 256]) for i in range(2)]
        p_o = ps("p_o", [128, 512])
        B_ptr = nb("ptr")
        B_pz = [[nb("pz%d" % i)] for i in range(2)]
        _bpcv, _bpst = nb("pcv"), nb("pst")
        B_pcv = [_bpcv, _bpcv]
        B_pst = [_bpst, _bpst]
        B_ps = [nb("ps%d" % i) for i in range(2)]
        B_pob = nb("po")
        B_po = [B_pob, _bpcv]

        for (t, src, nm) in ((identb, c_identb, "c0"), (identf, c_identf, "c1"), (utri, c_utri, "c2"),
                             (onesf, c_onesf, "c3"), (cmask, c_mask, "c4")):
            op("sp", lambda e, t=t, src=src: e.dma_start(out=t[:], in_=src[:, :]), writes=[B_const], lane=nm)

        def consts_init(e):
            e.memset(inv512[:], 1.0 / 512)
            e.memset(oneb[:], 1.0)
            e.memset(epst[:], EPS)
            return e.memset(onec[:], 1.0)
        B_c2 = nb("c2")
        op("pool", consts_init, writes=[B_c2])

        def sp(n=2):
            for _ in range(n):
                yield

        def norm_tile_g(slot, gt, B_g, src_t, B_src, dst_hT, B_dst, spaced=False):
            op("act", lambda e: e.activation(out=junk[:], in_=src_t[:], func=AF.Square, scale=1.0 / 32,
                                             accum_out=ss[slot][:]),
               reads=[B_src], writes=[B_junk, B_ss[slot]])
            yield
            if spaced:
                yield from sp(1)
            op("act", lambda e: e.activation(out=lnt[slot][:], in_=ss[slot][:], func=AF.Ln, bias=epst[:, 0:1]),
               reads=[B_ss[slot], B_c2], writes=[B_lnt[slot]])
            op("act", lambda e: e.activation(out=rstd[slot][:], in_=lnt[slot][:], func=AF.Exp, scale=-0.5),
               reads=[B_lnt[slot]], writes=[B_rstd[slot]])
            yield
            if spaced:
                yield from sp(1)
            op("dve", lambda e: e.scalar_tensor_tensor(out=hb[slot][:], in0=src_t[:], scalar=rstd[slot][:, 0:1],
                                                       in1=gt[:], op0=ALU.mult, op1=ALU.mult),
               reads=[B_src, B_rstd[slot], B_g], writes=[B_hb[slot]])
            yield
            if spaced:
                yield from sp(2)

            def tr(e):
                ins = None
                for c in range(KC):
                    ins = e.transpose(out=p_tr[:, c, :], in_=hb[slot][:, c * 128:(c + 1) * 128], identity=identb[:])
                return ins
            op("pe", tr, reads=[B_hb[slot], B_const], writes=[B_ptr])
            op("dve", lambda e: e.tensor_copy(out=dst_hT, in_=p_tr[:]), reads=[B_ptr], writes=[B_dst])
            yield

        def norm_tile(*a):
            for _ in norm_tile_g(*a):
                pass

        def wload(dst, src2d, kchunks, B_w, lane, extra_writes=(), after=()):
            sv = src2d.rearrange("(c p) n -> p c n", p=128)
            for c in range(kchunks):
                op("pool", lambda e, c=c: e.dma_start(out=dst[:, c, :], in_=sv[:, c, :]),
                   reads=list(after), writes=[B_w] + list(extra_writes), lane=lane)

        def bload(dst, vec, B_g, lane):
            op("sp", lambda e: e.dma_start(out=dst[:], in_=vec.partition_broadcast(128)), writes=[B_g], lane=lane)

        A = Arena(nc, PBASE)
        win = A.alloc("win", [128, KC, DIN], BF16)
        A_WOUT_OFF = A.off
        wout = A.alloc("wout", [128, KC, D], BF16)
        KT = A.alloc("KT", [65, 8, S], BF16)
        A_VT_OFF = A.off
        Vt = A.alloc("Vt", [128, NT, 8, 65], BF16)
        Dg = A.alloc("Dg", [128, 4, CK, 128], BF16)
        A_DG_END = A.off
        gmix = A.alloc("gmix", [128, D], F32)
        cw = A.alloc("cw", [128, 4, CK], F32)
        cb = A.alloc("cb", [128, 4], F32)
        lg_ = A.alloc("lg_", [128, 4], F32)
        lb_ = A.alloc("lb_", [128, 4], F32)
        nlb = A.alloc("nlb", [128, 4], F32)
        bft = A.alloc("bft", [128, 8], F32)
        hT = A.alloc("hT", [128, KC, TG], BF16)
        QT = A.alloc("QT", [65, 8, TG], BF16)
        a_t = A.alloc("a_t", [128, 4, 30 + TG], BF16)
        sg = [A.alloc("sg", [128, TG], F32) for _ in range(4)]
        ybf = A.alloc("ybf", [128, 4, TG], BF16)
        ysq = A.alloc("ysq", [128, 4, TG], BF16)
        msb = A.alloc("msb", [128, TG], F32)
        var_t = A.alloc("var_t", [128, TG], F32)
        rs_t = A.alloc("rs_t", [128, TG], F32)
        tmpd = [A.alloc("tmpd", [128, TG], F32) for _ in range(4)]
        mixT = A.alloc("mixT", [128, KC, TG], BF16)
        NPT = 6
        PT = [A.alloc("PT", [128, TG], BF16) for _ in range(NPT)]
        att = [A.alloc("att", [128, 512], BF16) for _ in range(TPG)]
        cum = A.alloc("cum", [128, NT, 8], F32)
        lgsum = A.alloc("lgsum", [128, 8], F32)
        f1 = A.alloc("f1", [128, 8], F32)
        f2 = A.alloc("f2", [128, 8], F32)
        lgt = A.alloc("lgt", [128, 8], F32)
        rT = A.alloc("rT", [8, TG], BF16)
        rec = [A.alloc("rec", [128, 2], F32) for _ in range(3)]
        print("phase A sbuf end", A.off)

        B_win, B_wout, B_gmix, B_cst = nb("win"), nb("wout"), nb("gmix"), nb("cst")
        B_Dg = nb("Dg")
        B_KT = S_.bufs(8, "KT")
        B_KT1 = nb("KT1")
        B_V = S_.bufs(NT, "V")
        B_V1 = nb("V1")
        B_hT = S_.bufs(TPG, "hT")
        B_QTq = S_.bufs(8, "QTq")
        B_QTr = nb("QTr")
        B_nlb = nb("nlb")
        B_a = S_.bufs(4, "a")
        B_sg = S_.bufs(4, "sg")
        B_ybf = S_.bufs(4, "ybf")
        B_ysq = S_.bufs(4, "ysq")
        B_msb, B_var, B_rs = nb("msb"), nb("var"), nb("rs")
        B_tmpd = S_.bufs(4, "tmpd")
        B_mixc = S_.bufs(4, "mixc")
        B_mixa = S_.bufs(TPG, "mixa")
        B_PT = S_.bufs(NPT, "PT")
        B_att = [[nb("att%d_%d" % (t, h)) for h in range(8)] for t in range(TPG)]
        B_cum = S_.bufs(NT, "cum")
        B_lgsum, B_f1, B_f2, B_lgt, B_rT = nb("lgsum"), nb("f1"), nb("f2"), nb("lgt"), nb("rT")
        B_rec = S_.bufs(3, "rec")
        B_x1 = [[nb("x1_%d_%d" % (b, t)) for t in range(NT)] for b in range(NB)]
        B_x2 = [[nb("x2_%d_%d" % (b, t)) for t in range(NT)] for b in range(NB)]

        bload(gmix, g_mix, B_gmix, "gmix")
        for (t, src, nm) in ((cw, convw, "k0"), (cb, convb, "k1"), (lg_, lng, "k2"), (lb_, lnb, "k3")):
            op("sp", lambda e, t=t, src=src: e.dma_start(out=t[:], in_=src), writes=[B_cst], lane=nm)
        op("sp", lambda e: e.dma_start(out=bft[:], in_=b_f.partition_broadcast(128)), writes=[B_cst], lane="k4")
        wload(win, w_in, KC, B_win, "win", after=[B_const, B_gmix, B_cst])
        wload(wout, w_out, KC, B_wout, "wout", after=[B_win])

        def mk_dg(e):
            ins = None
            for c in range(4):
                for k in range(CK):
                    ins = e.tensor_scalar(out=Dg[:, c, k, :], in0=identf[:], scalar1=cw[:, c, k:k + 1],
                                          scalar2=None, op0=ALU.mult)
            return ins

        op("pool", lambda e: e.tensor_scalar(out=nlb[:], in0=lb_[:], scalar1=-1.0, scalar2=None, op0=ALU.mult),
           reads=[B_cst], writes=[B_nlb])

        op("sp", lambda e: e.dma_start(out=KT[64:65, :, :], in_=c_onesb.rearrange("o (h s) -> o h s", h=8)),
           writes=[B_KT1], lane="kt1")
        op("pool", lambda e: e.memset(Vt[:, :, :, 64:65], 1.0), writes=[B_V1])

        zslot = [0]

        def next_z():
            i = zslot[0] % 2
            zslot[0] += 1
            return p_z[i][:, 0:256], B_pz[i][0]

        evac_rr = [0]

        def evac(out_ap, in_ap, reads, writes):
            evac_rr[0] += 1
            if evac_rr[0] % 2:
                op("act", lambda e: e.copy(out=out_ap, in_=in_ap), reads=reads, writes=writes)
            else:
                op("dve", lambda e: e.tensor_copy(out=out_ap, in_=in_ap), reads=reads, writes=writes)

        def proj_fm(col0, m, out_ps, B_out):
            def f(e):
                ins = None
                for kc in range(KC):
                    ins = e.matmul(out_ps, lhsT=win[:, kc, col0:col0 + m], rhs=hT[:, kc, :],
                                   start=(kc == 0), stop=(kc == KC - 1))
                return ins
            op("pe", f, reads=[B_win] + B_hT, writes=[B_out])

        tile_ctr = [0]
        pt_ctr = [0]
        po_ctr = [0]
        s_ctr = [0]
        z_ctr = [0]
        p_cv_flat = p_cv[:, :, :].rearrange("p a b -> p (a b)")
        zb = [(p_z[0], B_pz[0][0]), (p_z[1], B_pz[1][0]), (p_cv_flat, _bpcv), (p_o, B_pob)]

        def zbank():
            i = z_ctr[0] % 4
            z_ctr[0] += 1
            return zb[i]

        def run(gen):
            for _ in gen:
                pass

        def interleave(gens, weights):
            live = [(g_, w) for g_, w in zip(gens, weights) if g_ is not None]
            while live:
                nxt = []
                for g_, w in live:
                    alive = True
                    for _ in range(w):
                        try:
                            next(g_)
                        except StopIteration:
                            alive = False
                            break
                    if alive:
                        nxt.append((g_, w))
                live = nxt

        def prep(b, g):
            tok0 = g * TG
            for t in range(TPG):
                slot = tile_ctr[0] % 2
                tile_ctr[0] += 1
                tk = tok0 + t * 128
                op("sp", lambda e, slot=slot, tk=tk, b=b: e.dma_start(out=xin[slot][:], in_=x[b, tk:tk + 128, :]),
                   writes=[B_xin[slot]], lane="xin%d" % slot)
                yield from norm_tile_g(slot, gmix, B_gmix, xin[slot], B_xin[slot],
                                       hT[:, :, t * 128:(t + 1) * 128], B_hT[t], spaced=True)

        def fchain(b, g):
            tiles = [g * TPG + t for t in range(TPG)]
            for t in range(TPG):
                j = tiles[t]
                pf = p_st[:, 0, 0:8]

                def ff(e, t=t, pf=pf):
                    ins = None
                    for kc in range(KC):
                        ins = e.matmul(pf, lhsT=hT[:, kc, t * 128:(t + 1) * 128],
                                       rhs=win[:, kc, OFF_F:OFF_F + 8], start=(kc == 0), stop=(kc == KC - 1))
                    return ins
                op("pe", ff, reads=[B_win, B_hT[t]], writes=[B_pst[0]])
                op("dve", lambda e, pf=pf: e.tensor_tensor(out=f1[:], in0=pf, in1=bft[:], op=ALU.add),
                   reads=[B_pst[0], B_cst], writes=[B_f1])
                yield
                op("act", lambda e: e.activation(out=f2[:], in_=f1[:], func=AF.Exp, scale=-1.0),
                   reads=[B_f1], writes=[B_f2])
                op("act", lambda e: e.activation(out=lgt[:], in_=f2[:], func=AF.Ln, bias=onec[:, 0:1]),
                   reads=[B_f2, B_c2], writes=[B_lgt])
                yield
                pc = p_st[:, 0, 16:24]

                def fc(e, pc=pc):
                    e.matmul(pc, lhsT=utri[:], rhs=lgt[:], start=True, stop=False)
                    return e.matmul(pc, lhsT=onesf[:], rhs=lgsum[:], start=False, stop=True)
                op("pe", fc, reads=[B_lgt, B_lgsum, B_const], writes=[B_pst[0]])
                op("dve", lambda e, j=j, pc=pc: e.tensor_copy(out=cum[:, j, :], in_=pc),
                   reads=[B_pst[0]], writes=[B_cum[j]])
                op("pool", lambda e: e.tensor_tensor(out=lgsum[:], in0=lgsum[:], in1=lgt[:], op=ALU.add),
                   reads=[B_lgsum, B_lgt], writes=[B_lgsum])
                yield
                pr = p_st[0:8, 1, 0:128]
                op("pe", lambda e, j=j, pr=pr: e.transpose(out=pr, in_=cum[:, j, :], identity=identf[:]),
                   reads=[B_cum[j], B_const], writes=[B_pst[1]])
                op("act", lambda e, t=t, pr=pr: e.activation(out=rT[0:8, t * 128:(t + 1) * 128], in_=pr,
                                                            func=AF.Copy, scale=-8.0),
                   reads=[B_pst[1]], writes=[B_rT])
                yield
            for h in range(8):
                op("sp", lambda e, h=h: e.dma_start(out=QT[64:65, h, :], in_=rT[h:h + 1, :]),
                   reads=[B_rT], writes=[B_QTr], lane="qtr")
            yield

        def projmain(b, g):
            tok0 = g * TG
            tiles = [g * TPG + t for t in range(TPG)]
            for hp in range(4):
                for kind in range(2):
                    pz_, Bz = zbank()
                    proj_fm((OFF_Q, OFF_K)[kind] + hp * 128, 128, pz_[:, 0:TG], Bz)
                    for hh in range(2):
                        h = 2 * hp + hh
                        src = pz_[hh * 64:(hh + 1) * 64, 0:TG]
                        if kind == 0:
                            dst, Bd = QT[0:64, h, :], B_QTq[h]
                        else:
                            dst, Bd = KT[0:64, h, tok0:tok0 + TG], B_KT[h]
                        if hh == 0:
                            op("act", lambda e, dst=dst, src=src: e.copy(out=dst, in_=src), reads=[Bz], writes=[Bd])
                        else:
                            op("dve", lambda e, dst=dst, src=src: e.tensor_copy(out=dst, in_=src), reads=[Bz], writes=[Bd])
                    yield
            for t in range(TPG):
                j = tiles[t]
                pv, Bv = zbank()

                def fv(e, t=t, pv=pv):
                    ins = None
                    for kc in range(KC):
                        ins = e.matmul(pv[:, :], lhsT=hT[:, kc, t * 128:(t + 1) * 128],
                                       rhs=win[:, kc, OFF_V:OFF_V + 512], start=(kc == 0), stop=(kc == KC - 1))
                    return ins
                op("pe", fv, reads=[B_win, B_hT[t]], writes=[Bv])
                evac(Vt[:, j, :, 0:64], pv[:, :].rearrange("p (h d) -> p h d", h=8), [Bv], [B_V[j]])
                yield
            for c in range(4):
                pu, Bu = zbank()
                proj_fm(OFF_U + c * 128, 128, pu[:, 0:TG], Bu)
                pg, Bg = zbank()
                proj_fm(OFF_G + c * 128, 128, pg[:, 0:TG], Bg)
                sl = c % 2
                op("act", lambda e, pg=pg, sl=sl: e.activation(out=sg[sl][:], in_=pg[:, 0:TG], func=AF.Exp, scale=-1.0),
                   reads=[Bg], writes=[B_sg[sl]])
                op("dve", lambda e, sl=sl: e.tensor_scalar(out=sg[sl][:], in0=sg[sl][:], scalar1=1.0, scalar2=None,
                                                           op0=ALU.add),
                   reads=[B_sg[sl]], writes=[B_sg[sl]])
                op("dve", lambda e, sl=sl: e.reciprocal(out=sg[sl][:], in_=sg[sl][:]),
                   reads=[B_sg[sl]], writes=[B_sg[sl]])
                op("dve", lambda e, pu=pu, sl=sl, c=c: e.tensor_tensor(out=a_t[:, c, 30:30 + TG], in0=pu[:, 0:TG],
                                                                      in1=sg[sl][:], op=ALU.mult),
                   reads=[Bu, B_sg[sl]], writes=[B_a[c]])
                yield

        def convchain(b, g):
            NPART = 4
            for c in range(4):
                pcv, Bcv = p_z[0][:, (c % 2) * TG:(c % 2) * TG + TG], B_pz[0][0]
                for part in range(NPART):
                    k0, k1 = part * 8, min(CK, part * 8 + 8)

                    def fcv(e, c=c, pcv=pcv, k0=k0, k1=k1):
                        ins = None
                        for k in range(k0, k1):
                            ins = e.matmul(pcv, lhsT=Dg[:, c, k, :], rhs=a_t[:, c, k:k + TG],
                                           start=(k == 0), stop=(k == CK - 1))
                        return ins
                    op("pe", fcv, reads=[B_Dg, B_a[c]], writes=[Bcv])
                    yield
                yield from sp(2)
                op("act", lambda e, c=c, pcv=pcv: e.activation(out=ybf[:, c, :], in_=pcv, func=AF.Identity,
                                                              bias=cb[:, c:c + 1]),
                   reads=[Bcv, B_cst], writes=[B_ybf[c]])
                yield from sp(2)
                op("pool", lambda e, c=c: e.tensor_tensor(out=ysq[:, c, :], in0=ybf[:, c, :], in1=ybf[:, c, :], op=ALU.mult),
                   reads=[B_ybf[c]], writes=[B_ysq[c]])
                yield
            op("pool", lambda e: e.tensor_copy(out=a_t[:, :, 0:30], in_=a_t[:, :, TG:TG + 30]),
               reads=B_a, writes=B_a)
            yield from sp(2)

            def fst(e):
                ins = None
                for c in range(4):
                    ins = e.matmul(p_st[:, 0, :], lhsT=inv512[:], rhs=ybf[:, c, :], start=(c == 0), stop=(c == 3))
                for c in range(4):
                    ins = e.matmul(p_st[:, 1, :], lhsT=inv512[:], rhs=ysq[:, c, :], start=(c == 0), stop=(c == 3))
                return ins
            op("pe", fst, reads=B_ybf + B_ysq + [B_c2], writes=B_pst)
            yield from sp(3)
            op("act", lambda e: e.copy(out=msb[:], in_=p_st[:, 0, :]), reads=[B_pst[0]], writes=[B_msb])
            yield from sp(2)
            op("dve", lambda e: e.tensor_tensor(out=var_t[:], in0=msb[:], in1=msb[:], op=ALU.mult),
               reads=[B_msb], writes=[B_var])
            op("dve", lambda e: e.tensor_tensor(out=var_t[:], in0=p_st[:, 1, :], in1=var_t[:], op=ALU.subtract),
               reads=[B_pst[1], B_var], writes=[B_var])
            yield from sp(2)
            op("act", lambda e: e.activation(out=rs_t[:], in_=var_t[:], func=AF.Ln, bias=epst[:, 0:1]),
               reads=[B_var, B_c2], writes=[B_rs])
            op("act", lambda e: e.activation(out=rs_t[:], in_=rs_t[:], func=AF.Exp, scale=-0.5),
               reads=[B_rs], writes=[B_rs])
            yield from sp(2)
            for c in range(4):
                op("dve", lambda e, c=c: e.tensor_tensor(out=tmpd[c][:], in0=ybf[:, c, :], in1=msb[:], op=ALU.subtract),
                   reads=[B_ybf[c], B_msb], writes=[B_tmpd[c]])
                op("dve", lambda e, c=c: e.scalar_tensor_tensor(out=tmpd[c][:], in0=tmpd[c][:], scalar=lg_[:, c:c + 1],
                                                               in1=rs_t[:], op0=ALU.mult, op1=ALU.mult),
                   reads=[B_tmpd[c], B_rs, B_cst], writes=[B_tmpd[c]])
                yield
            yield from sp(1)
            for c in range(4):
                op("act", lambda e, c=c: e.activation(out=sg[c][:], in_=tmpd[c][:], func=AF.Exp,
                                                      bias=nlb[:, c:c + 1], scale=-1.0),
                   reads=[B_tmpd[c], B_nlb], writes=[B_sg[c]])
                yield
            yield from sp(1)
            for c in range(4):
                op("dve", lambda e, c=c: e.tensor_scalar(out=sg[c][:], in0=sg[c][:], scalar1=1.0, scalar2=None, op0=ALU.add),
                   reads=[B_sg[c]], writes=[B_sg[c]])
                op("dve", lambda e, c=c: e.reciprocal(out=sg[c][:], in_=sg[c][:]),
                   reads=[B_sg[c]], writes=[B_sg[c]])
                op("dve", lambda e, c=c: e.scalar_tensor_tensor(out=mixT[:, c, :], in0=tmpd[c][:], scalar=lb_[:, c:c + 1],
                                                               in1=sg[c][:], op0=ALU.add, op1=ALU.mult),
                   reads=[B_tmpd[c], B_sg[c], B_cst], writes=[B_mixc[c]])
                yield

        def attn(b, g):
            tiles = [g * TPG + t for t in range(TPG)]
            i_last = tiles[-1]
            steps = [(h, j) for h in range(8) for j in range(i_last + 1)]
            info = {}

            def emit_qk(s):
                h, j = steps[s]
                t_lo = max(0, j - tiles[0])
                q0 = t_lo * 128
                si = s_ctr[0] % 3
                s_ctr[0] += 1
                sbank = (p_s[0][:, 0, :], p_s[1][:, 0, :], p_z[1][:, 0:TG])[si]
                psv = sbank[:, q0:TG]
                diag = j >= tiles[0]

                def fqk(e, h=h, j=j, q0=q0, psv=psv, diag=diag, sbank=sbank):
                    ins = e.matmul(psv, lhsT=KT[0:65, h, j * 128:(j + 1) * 128], rhs=QT[0:65, h, q0:TG],
                                   start=True, stop=not diag)
                    if diag:
                        ins = e.matmul(sbank[:, q0:q0 + 128], lhsT=identb[:], rhs=cmask[:],
                                       start=False, stop=True)
                    return ins
                op("pe", fqk, reads=[B_KT[h], B_KT1, B_QTq[h], B_QTr, B_const], writes=[B_s3[si]])
                info[s] = (t_lo, q0, si, psv)

            B_s3 = [B_ps[0], B_ps[1], B_pz[1][0]]
            emit_qk(0)
            if len(steps) > 1:
                emit_qk(1)
            pos = 0
            po_v = None
            for s in range(len(steps)):
                h, j = steps[s]
                if j == 0:
                    pos = po_ctr[0] % 2
                    po_ctr[0] += 1
                    po_v = (p_o[:, 0:130], p_cv[:, 0, 0:130])[pos].rearrange("p (t d) -> p t d", t=TPG)
                if s + 2 < len(steps):
                    emit_qk(s + 2)
                t_lo, q0, si, psv = info.pop(s)
                pi = pt_ctr[0] % NPT
                pt_ctr[0] += 1
                op("act", lambda e, h=h, j=j, q0=q0, psv=psv, pi=pi: e.activation(
                    out=PT[pi][:, q0:TG], in_=psv, func=AF.Exp, bias=cum[:, j, h:h + 1], scale=0.125),
                   reads=[B_s3[si], B_cum[j]], writes=[B_PT[pi]])

                def fpv(e, h=h, j=j, t_lo=t_lo, pi=pi, po_v=po_v):
                    ins = None
                    for t in range(t_lo, TPG):
                        ins = e.matmul(po_v[:, t, :], lhsT=PT[pi][:, t * 128:(t + 1) * 128],
                                       rhs=Vt[:, j, h, :], start=(j == 0 and t == 0), stop=(j == tiles[t]),
                                       skip_group_check=True)
                    return ins
                op("pe", fpv, reads=[B_PT[pi], B_V[j], B_V1], writes=[B_po[pos]])
                if j == i_last:
                    op("dve", lambda e, pos=pos, po_v=po_v: e.reciprocal(out=rec[pos][:, 0:TPG], in_=po_v[:, :, 64]),
                       reads=[B_po[pos]], writes=[B_rec[pos]])
                    for t in range(TPG):
                        op("dve", lambda e, t=t, h=h, pos=pos, po_v=po_v: e.tensor_scalar(
                            out=att[t][:, h * 64:(h + 1) * 64], in0=po_v[:, t, 0:64], scalar1=rec[pos][:, t:t + 1],
                            scalar2=None, op0=ALU.mult),
                           reads=[B_po[pos], B_rec[pos]], writes=[B_att[t][h]])
                yield

        def outproj1(b, g):
            for t in range(TPG):
                def tra(e, t=t):
                    ins = None
                    for c in range(4):
                        ins = e.transpose(out=p_tr[:, c, :], in_=att[t][:, c * 128:(c + 1) * 128], identity=identb[:])
                    return ins
                op("pe", tra, reads=B_att[t] + [B_const], writes=[B_ptr])
                op("dve", lambda e, t=t: e.tensor_copy(out=mixT[:, 4:8, t * 128:(t + 1) * 128], in_=p_tr[:, 0:4, :]),
                   reads=[B_ptr], writes=[B_mixa[t]])
                yield

        def outproj2(b, g):
            tok0 = g * TG
            tiles = [g * TPG + t for t in range(TPG)]
            pw2 = [p_s[0][:, :, :].rearrange("p a b -> p (a b)"), p_s[1][:, :, :].rearrange("p a b -> p (a b)")]
            for t in range(TPG):
                j = tiles[t]
                tk = tok0 + t * 128
                rsl = j % 2
                op("sp", lambda e, rsl=rsl, tk=tk, b=b: e.dma_start(out=xres[rsl][:], in_=x[b, tk:tk + 128, :]),
                   writes=[B_xres[rsl]], lane="xres%d" % rsl)
                for half in range(2):
                    def fo(e, t=t, half=half):
                        ins = None
                        for kc in range(KC):
                            ins = e.matmul(pw2[half], lhsT=mixT[:, kc, t * 128:(t + 1) * 128],
                                           rhs=wout[:, kc, half * 512:(half + 1) * 512],
                                           start=(kc == 0), stop=(kc == KC - 1))
                        return ins
                    op("pe", fo, reads=B_mixc + [B_mixa[t], B_wout], writes=[B_ps[half]])
                    op("dve", lambda e, rsl=rsl, half=half: e.tensor_tensor(
                        out=xres[rsl][:, half * 512:(half + 1) * 512], in0=pw2[half],
                        in1=xres[rsl][:, half * 512:(half + 1) * 512], op=ALU.add),
                       reads=[B_ps[half], B_xres[rsl]], writes=[B_xres[rsl]])
                    yield
                op("sp", lambda e, rsl=rsl, tk=tk, b=b: e.dma_start(out=x1s[b, tk:tk + 128, :], in_=xres[rsl][:]),
                   reads=[B_xres[rsl]], writes=[B_x1[b][j]], lane="xres%d" % rsl)
                yield

        Bm = Arena(nc, PBASE)
        wmq = Bm.alloc("wmq", [128, KC, D], BF16)
        wmo = Bm.alloc("wmo", [128, KC, D], BF16)
        GU_OFF = SBUF_LIMIT - KC * 2 * DFF * 2
        Wk = Arena(nc, GU_OFF)
        wmkv = Wk.alloc("wmkv", [128, KC, 2 * D], BF16)
        B_wmq, B_wmo, B_wmkv, B_wgu = nb("wmq"), nb("wmo"), nb("wmkv"), nb("wgu")
        assert Bm.off <= A_WOUT_OFF and GU_OFF >= A_VT_OFF and GU_OFF + KC * 2 * D * 2 <= A_DG_END

        groups = [(b, g) for b in range(nb_a) for g in range(ng_a)]
        run(prep(*groups[0]))
        op("dve", mk_dg, reads=[B_const, B_cst], writes=[B_Dg])
        for idx, (b, g) in enumerate(groups):
            if g == 0:
                op("pool", lambda e: e.memset(a_t[:, :, 0:30], 0.0), writes=B_a)
                op("pool", lambda e: e.memset(lgsum[:], 0.0), writes=[B_lgsum])
            prev_out = outproj2(*groups[idx - 1]) if idx > 0 else None
            interleave([fchain(b, g), projmain(b, g), prev_out], [1, 1, 1])
            last = idx == len(groups) - 1 and stop is None
            if last:
                wload(wmq, w_mq, KC, B_wmq, "wmq", extra_writes=[B_win])
                wload(wmo, w_mo, KC, B_wmo, "wmo", extra_writes=[B_win])
            nxt = prep(*groups[idx + 1]) if idx + 1 < len(groups) else None

            def attn_tr(b=b, g=g):
                yield from attn(b, g)
                yield from outproj1(b, g)
            interleave([attn_tr(), convchain(b, g), nxt], [1, 1, 1])
            if last:
                wload(wmkv, w_mkv, KC, B_wmkv, "wmkv", extra_writes=B_V + [B_V1, B_Dg])
        run(outproj2(*groups[-1]))

        if stop == "A":
            S_.run(final_lanes=list(S_.lane_counts))
            return nc
        S_.barrier(exclude_lanes=("wmq", "wmo", "wmkv"))
        mKT = [Bm.alloc("mKT", [128, 8, MEM], BF16) for _ in range(NB)]
        mV = [Bm.alloc("mV", [128, 2, D], BF16) for _ in range(NB)]
        gxt = Bm.alloc("gxt", [128, D], F32)
        gmt = Bm.alloc("gmt", [128, D], F32)
        NSB = 3
        hTb = [Bm.alloc("hTb", [128, KC, 128], BF16) for _ in range(NSB)]
        qTb = [Bm.alloc("qTb", [128, 8, 128], BF16) for _ in range(NSB)]
        PTb = [Bm.alloc("PTb", [128, 8, 128], BF16) for _ in range(NSB)]
        onb = [Bm.alloc("onb", [128, D], BF16) for _ in range(NSB)]
        oTb = [Bm.alloc("oTb", [128, KC, 128], BF16) for _ in range(NSB)]
        rdn = [Bm.alloc("rdn", [128, 4], F32) for _ in range(NSB)]
        xin.append(Bm.alloc("xin2", [128, D], F32))
        hb.append(Bm.alloc("hb2", [128, D], BF16))
        B_xin.append(nb("xin2"))
        B_hb.append(nb("hb2"))
        print("phase B sbuf end", Bm.off)
        assert Bm.off <= GU_OFF
        Wg = Arena(nc, GU_OFF)
        wgu = Wg.alloc("wgu", [128, KC, 2 * DFF], BF16)

        B_gx, B_gm = nb("gx"), nb("gm")
        B_mKT = [[nb("mKT%d_%d" % (b, c)) for c in range(16)] for b in range(NB)]
        B_mV = [[nb("mV%d_%d" % (b, c)) for c in range(4)] for b in range(NB)]
        B_hTb, B_qTb, B_PTb, B_onb, B_oTb, B_rdn = (S_.bufs(NSB, n) for n in ("hTb", "qTb", "PTb", "onb", "oTb", "rdn"))
        B_qTb = [[nb("qTb%d_%d" % (s, c)) for c in range(2)] for s in range(NSB)]
        B_PTb = [[nb("PTb%d_%d" % (s, c)) for c in range(2)] for s in range(NSB)]
        B_onbh = [[nb("onb%d_%d" % (s, c)) for c in range(4)] for s in range(NSB)]

        if stop is not None:
            wload(wmkv, w_mkv, KC, B_wmkv, "wmkv")
            wload(wmq, w_mq, KC, B_wmq, "wmq", after=[B_wmkv])
            wload(wmo, w_mo, KC, B_wmo, "wmo", after=[B_wmq])
        bload(gmt, g_mem, B_gm, "gmt")
        bload(gxt, g_x, B_gx, "gxt")

        p_q = p_z[0]
        p_l = p_z[1]
        B_pq, B_pl = nb("pq"), nb("pl")
        B_pob = S_.bufs(2, "pob")
        B_pwo = S_.bufs(2, "pwo")
        B_pden = nb("pden")
        po_b = [p_cv, p_st]
        pwo_b = p_s

        def mtile(idx):
            b, mt = divmod(idx, 2)
            slot = idx % NSB
            op("sp", lambda e, slot=slot, mt=mt, b=b: e.dma_start(out=xin[slot][:], in_=mem[b, mt * 128:(mt + 1) * 128, :]),
               writes=[B_xin[slot]], lane="xin%d" % slot)
            yield from norm_tile_g(slot, gmt, B_gm, xin[slot], B_xin[slot], hTb[slot][:, :, :], B_hTb[slot])
            for c4 in range(2):
                def fk(e, c4=c4, slot=slot):
                    ins = None
                    for cc in range(4):
                        c = c4 * 4 + cc
                        for kc in range(KC):
                            ins = e.matmul(p_q[:, cc * 128:(cc + 1) * 128], lhsT=wmkv[:, kc, c * 128:(c + 1) * 128],
                                           rhs=hTb[slot][:, kc, :], start=(kc == 0), stop=(kc == KC - 1))
                    return ins
                op("pe", fk, reads=[B_wmkv, B_hTb[slot]], writes=[B_pq])
                evac(mKT[b][:, c4 * 4:(c4 + 1) * 4, mt * 128:(mt + 1) * 128],
                     p_q[:, :].rearrange("p (c m) -> p c m", c=4), [B_pq], [B_mKT[b][mt * 2 + c4]])
                yield
            for half in range(2):
                def fmv(e, half=half, slot=slot):
                    ins = None
                    for kc in range(KC):
                        ins = e.matmul(p_l[:, :], lhsT=hTb[slot][:, kc, :],
                                       rhs=wmkv[:, kc, D + half * 512:D + (half + 1) * 512],
                                       start=(kc == 0), stop=(kc == KC - 1))
                    return ins
                op("pe", fmv, reads=[B_wmkv, B_hTb[slot]], writes=[B_pl])
                evac(mV[b][:, mt, half * 512:(half + 1) * 512], p_l[:, :], [B_pl], [B_mV[b][mt * 2 + half]])
                yield

        def pipeline(gens, lag):
            gens = list(gens)
            live = []
            tick = 0
            nxt_i = 0
            last_start = -lag
            while live or nxt_i < len(gens):
                if nxt_i < len(gens) and tick - last_start >= lag and len(live) < NSB:
                    live.append(gens[nxt_i])
                    nxt_i += 1
                    last_start = tick
                keep = []
                for g_ in live:
                    try:
                        next(g_)
                        keep.append(g_)
                    except StopIteration:
                        pass
                live = keep
                tick += 1

        pipeline([mtile(i) for i in range(2 * NB)], lag=3)
        wload(wgu, w_gu, KC, B_wgu, "wgu", extra_writes=[B_wmkv])

        def btile(ti):
            b, j = divmod(ti, NT)
            slot = ti % NSB
            tk = j * 128
            op("sp", lambda e, slot=slot, tk=tk, b=b: e.dma_start(out=xin[slot][:], in_=x1s[b, tk:tk + 128, :]),
               reads=[B_x1[b][j]], writes=[B_xin[slot]], lane="xin%d" % slot)
            yield from norm_tile_g(slot, gxt, B_gx, xin[slot], B_xin[slot], hTb[slot][:, :, :], B_hTb[slot])
            for c4 in range(2):
                def fq(e, c4=c4, slot=slot):
                    ins = None
                    for cc in range(4):
                        c = c4 * 4 + cc
                        for kc in range(KC):
                            ins = e.matmul(p_q[:, cc * 128:(cc + 1) * 128], lhsT=wmq[:, kc, c * 128:(c + 1) * 128],
                                           rhs=hTb[slot][:, kc, :], start=(kc == 0), stop=(kc == KC - 1))
                    return ins
                op("pe", fq, reads=[B_wmq, B_hTb[slot]], writes=[B_pq])
                evac(qTb[slot][:, c4 * 4:(c4 + 1) * 4, :], p_q[:, :].rearrange("p (c m) -> p c m", c=4),
                     [B_pq], [B_qTb[slot][c4]])
                yield
            for hp in range(2):
                def fl(e, hp=hp, slot=slot, b=b):
                    ins = None
                    for hh in range(2):
                        hd = hp * 2 + hh
                        for mt in range(2):
                            o4 = hh * 2 + mt
                            for dc in range(2):
                                ins = e.matmul(p_l[:, o4 * 128:(o4 + 1) * 128],
                                               lhsT=mKT[b][:, hd * 2 + dc, mt * 128:(mt + 1) * 128],
                                               rhs=qTb[slot][:, hd * 2 + dc, :], start=(dc == 0), stop=(dc == 1))
                    return ins
                op("pe", fl, reads=B_mKT[b] + [B_qTb[slot][hp]], writes=[B_pl])
                op("act", lambda e, hp=hp, slot=slot: e.activation(
                    out=PTb[slot][:, hp * 4:(hp + 1) * 4, :], in_=p_l[:, :].rearrange("p (c m) -> p c m", c=4),
                    func=AF.Exp, scale=1.0 / 16),
                   reads=[B_pl], writes=[B_PTb[slot][hp]])
                yield
            for hd in range(4):
                hp, hh = divmod(hd, 2)
                pob = po_b[hd // 2]

                def fpv2(e, hd=hd, hp=hp, hh=hh, pob=pob, slot=slot, b=b):
                    ins = None
                    for mt in range(2):
                        ins = e.matmul(pob[:, hh, :], lhsT=PTb[slot][:, hp * 4 + hh * 2 + mt, :],
                                       rhs=mV[b][:, mt, hd * 256:(hd + 1) * 256], start=(mt == 0), stop=(mt == 1))
                    for mt in range(2):
                        ins = e.matmul(p_o[:, hd:hd + 1], lhsT=PTb[slot][:, hp * 4 + hh * 2 + mt, :],
                                       rhs=oneb[:, 0:1], start=(mt == 0), stop=(mt == 1))
                    return ins
                op("pe", fpv2, reads=[B_PTb[slot][hp], B_c2] + B_mV[b], writes=[B_pob[hd // 2], B_pden])
                if hd % 2:
                    yield
            op("dve", lambda e, slot=slot: e.reciprocal(out=rdn[slot][:], in_=p_o[:, 0:4]),
               reads=[B_pden], writes=[B_rdn[slot]])
            for hd in range(4):
                pob = po_b[hd // 2]
                eng = "dve"
                if eng == "dve":
                    op("dve", lambda e, hd=hd, pob=pob, slot=slot: e.tensor_scalar(
                        out=onb[slot][:, hd * 256:(hd + 1) * 256], in0=pob[:, hd % 2, :], scalar1=rdn[slot][:, hd:hd + 1],
                        scalar2=None, op0=ALU.mult),
                       reads=[B_pob[hd // 2], B_rdn[slot]], writes=[B_onbh[slot][hd]])
                else:
                    op("act", lambda e, hd=hd, pob=pob, slot=slot: e.activation(
                        out=onb[slot][:, hd * 256:(hd + 1) * 256], in_=pob[:, hd % 2, :], func=AF.Identity,
                        scale=rdn[slot][:, hd:hd + 1]),
                       reads=[B_pob[hd // 2], B_rdn[slot]], writes=[B_onbh[slot][hd]])
            yield

            def tro(e, slot=slot):
                ins = None
                for c in range(KC):
                    ins = e.transpose(out=p_tr[:, c, :], in_=onb[slot][:, c * 128:(c + 1) * 128], identity=identb[:])
                return ins
            op("pe", tro, reads=B_onbh[slot] + [B_const], writes=[B_ptr])
            op("dve", lambda e, slot=slot: e.tensor_copy(out=oTb[slot][:, :, :], in_=p_tr[:]),
               reads=[B_ptr], writes=[B_oTb[slot]])
            yield
            for half in range(2):
                pw = pwo_b[half]

                def fwo(e, half=half, slot=slot, pw=pw):
                    ins = None
                    for kc in range(KC):
                        ins = e.matmul(pw[:, :, :].rearrange("p a b -> p (a b)"), lhsT=oTb[slot][:, kc, :],
                                       rhs=wmo[:, kc, half * 512:(half + 1) * 512], start=(kc == 0), stop=(kc == KC - 1))
                    return ins
                op("pe", fwo, reads=[B_oTb[slot], B_wmo], writes=[B_pwo[half]])
                op("dve", lambda e, half=half, slot=slot, pw=pw: e.tensor_tensor(
                    out=xin[slot][:, half * 512:(half + 1) * 512], in0=pw[:, :, :].rearrange("p a b -> p (a b)"),
                    in1=xin[slot][:, half * 512:(half + 1) * 512], op=ALU.add),
                   reads=[B_pwo[half], B_xin[slot]], writes=[B_xin[slot]])
                yield
            op("sp", lambda e, slot=slot, tk=tk, b=b: e.dma_start(out=x2s[b, tk:tk + 128, :], in_=xin[slot][:]),
               reads=[B_xin[slot]], writes=[B_x2[b][j]], lane="xin%d" % slot)
            yield

        pipeline([btile(ti) for ti in range(nt_b)], lag=6)

        if stop == "B":
            S_.run(final_lanes=list(S_.lane_counts))
            return nc
        S_.barrier()
        C = Arena(nc, PBASE)
        wdn = C.alloc("wdn", [128, FJ, D], BF16)
        gft = C.alloc("gft", [128, D], F32)
        gfin = C.alloc("gfin", [128, D], F32)
        hTc = [C.alloc("hTc", [128, KC, TGC], BF16) for _ in range(2)]
        actT = C.alloc("actT", [128, FJ, TGC], BF16)
        slt = [C.alloc("slt", [128, TGC], F32) for _ in range(2)]
        print("phase C sbuf end", C.off)
        assert C.off <= GU_OFF
        B_wdn, B_gf, B_gfin = nb("wdn"), nb("gf"), nb("gfin")
        B_hTc = [S_.bufs(TGC // 128, "hTc%d_" % i) for i in range(2)]
        B_actT = S_.bufs(FJ, "actT")
        B_slt = S_.bufs(2, "slt")
        wload(wdn, w_down, FJ, B_wdn, "wdn")
        bload(gft, g_ffn, B_gf, "gft")
        bload(gfin, g_final, B_gfin, "gfin")
        gu_banks = [(p_z[0][:, :], p_z[1][:, :]),
                    (p_cv[:, :, :].rearrange("p a b -> p (a b)"), p_st[:, :, :].rearrange("p a b -> p (a b)"))]
        B_gub = [(nb("gb0"), nb("ub0")), (nb("gb1"), nb("ub1"))]
        pdn = [p_s[0][:, :, :].rearrange("p a b -> p (a b)"), p_s[1][:, :, :].rearrange("p a b -> p (a b)")]
        B_pdn = S_.bufs(2, "pdn")
        NTC = TGC // 128

        def cprep(gi):
            hs = gi % 2
            for t in range(NTC):
                ti = gi * NTC + t
                b, j = divmod(ti, NT)
                slot = ti % 2
                tk = j * 128
                op("sp", lambda e, slot=slot, tk=tk, b=b: e.dma_start(out=xin[slot][:], in_=x2s[b, tk:tk + 128, :]),
                   reads=[B_x2[b][j]], writes=[B_xin[slot]], lane="xin%d" % slot)
                yield from norm_tile_g(slot, gft, B_gf, xin[slot], B_xin[slot],
                                       hTc[hs][:, :, t * 128:(t + 1) * 128], B_hTc[hs][t], spaced=True)

        def cmain(gi):
            hs = gi % 2
            for jf in range(FJ):
                gb = jf % 2
                pg, pu = gu_banks[gb]
                Bg, Bu = B_gub[gb]

                def fgu(e, jf=jf, pg=pg, pu=pu, hs=hs):
                    ins = None
                    for kc in range(KC):
                        ins = e.matmul(pg, lhsT=wgu[:, kc, jf * 128:(jf + 1) * 128], rhs=hTc[hs][:, kc, :],
                                       start=(kc == 0), stop=(kc == KC - 1))
                    for kc in range(KC):
                        ins = e.matmul(pu, lhsT=wgu[:, kc, DFF + jf * 128:DFF + (jf + 1) * 128], rhs=hTc[hs][:, kc, :],
                                       start=(kc == 0), stop=(kc == KC - 1))
                    return ins
                op("pe", fgu, reads=[B_wgu] + B_hTc[hs], writes=[Bg, Bu])
                sl = jf % 2
                op("act", lambda e, pg=pg, sl=sl: e.activation(out=slt[sl][:], in_=pg, func=AF.Silu),
                   reads=[Bg], writes=[B_slt[sl]])
                op("dve", lambda e, pu=pu, sl=sl, jf=jf: e.tensor_tensor(out=actT[:, jf, :], in0=pu, in1=slt[sl][:],
                                                                        op=ALU.mult),
                   reads=[Bu, B_slt[sl]], writes=[B_actT[jf]])
                yield
            for t in range(NTC):
                ti = gi * NTC + t
                b, j = divmod(ti, NT)
                rsl = ti % 2
                tk = j * 128
                op("sp", lambda e, rsl=rsl, tk=tk, b=b: e.dma_start(out=xres[rsl][:], in_=x2s[b, tk:tk + 128, :]),
                   reads=[B_x2[b][j]], writes=[B_xres[rsl]], lane="xres%d" % rsl)
                for half in range(2):
                    def fd(e, t=t, half=half):
                        ins = None
                        for jf in range(FJ):
                            ins = e.matmul(pdn[half], lhsT=actT[:, jf, t * 128:(t + 1) * 128],
                                           rhs=wdn[:, jf, half * 512:(half + 1) * 512],
                                           start=(jf == 0), stop=(jf == FJ - 1))
                        return ins
                    op("pe", fd, reads=B_actT + [B_wdn], writes=[B_pdn[half]])
                    op("dve", lambda e, rsl=rsl, half=half: e.tensor_tensor(
                        out=xres[rsl][:, half * 512:(half + 1) * 512], in0=pdn[half],
                        in1=xres[rsl][:, half * 512:(half + 1) * 512], op=ALU.add),
                       reads=[B_pdn[half], B_xres[rsl]], writes=[B_xres[rsl]])
                    yield
                sl = 2 + rsl
                op("act", lambda e, rsl=rsl, sl=sl: e.activation(out=junk[:], in_=xres[rsl][:], func=AF.Square,
                                                                 scale=1.0 / 32, accum_out=ss[sl][:]),
                   reads=[B_xres[rsl]], writes=[B_junk, B_ss[sl]])
                op("act", lambda e, sl=sl: e.activation(out=lnt[sl][:], in_=ss[sl][:], func=AF.Ln, bias=epst[:, 0:1]),
                   reads=[B_ss[sl], B_c2], writes=[B_lnt[sl]])
                op("act", lambda e, sl=sl: e.activation(out=rstd[sl][:], in_=lnt[sl][:], func=AF.Exp, scale=-0.5),
                   reads=[B_lnt[sl]], writes=[B_rstd[sl]])
                yield
                op("dve", lambda e, rsl=rsl, sl=sl: e.scalar_tensor_tensor(
                    out=xres[rsl][:], in0=xres[rsl][:], scalar=rstd[sl][:, 0:1], in1=gfin[:],
                    op0=ALU.mult, op1=ALU.mult),
                   reads=[B_xres[rsl], B_rstd[sl], B_gfin], writes=[B_xres[rsl]])
                op("sp", lambda e, rsl=rsl, tk=tk, b=b: e.dma_start(out=y[b, tk:tk + 128, :], in_=xres[rsl][:]),
                   reads=[B_xres[rsl]], writes=[nb()], lane="xres%d" % rsl)
                yield

        if ng_c:
            run(cprep(0))
        for gi in range(ng_c):
            nxt = cprep(gi + 1) if gi + 1 < ng_c else None
            interleave([cmain(gi), nxt], [1, 1])

        S_.run(final_lanes=["xres0", "xres1", "xin0", "xin1"])
    return nc


_CACHE = {}


def _consts():
    bf = ml_dtypes.bfloat16
    ident = np.eye(128, dtype=np.float32)
    utri = np.triu(np.ones((128, 128), dtype=np.float32))
    mask = np.where(np.arange(128)[:, None] <= np.arange(128)[None, :], 0.0, -30000.0).astype(np.float32)
    return {
        "c_identb": ident.astype(bf),
        "c_identf": ident,
        "c_utri": utri,
        "c_onesf": np.ones((128, 128), dtype=np.float32),
        "c_mask": mask.astype(bf),
        "c_onesb": np.ones((1, 8 * S), dtype=bf),
    }


def _chan(v):
    return np.ascontiguousarray(np.asarray(v, dtype=np.float32).reshape(4, 128).T)


def make_in_maps(inputs):
    f = lambda k: np.ascontiguousarray(np.asarray(inputs[k], dtype=np.float32))
    shared = {
        "w_in": f("w_in")[0], "w_out": f("w_out")[0], "w_mq": f("w_mq")[0], "w_mkv": f("w_mkv")[0],
        "w_mo": f("w_mo")[0], "w_gu": f("w_gu")[0], "w_down": f("w_down")[0],
        "g_mix": f("g_mix")[0], "g_x": f("g_x")[0], "g_mem": f("g_mem"), "g_ffn": f("g_ffn")[0],
        "g_final": f("g_final"), "b_f": f("b_f")[0],
        "convw": np.ascontiguousarray(f("conv_w")[0].T.reshape(4, 128, CK).transpose(1, 0, 2)),
        "convb": _chan(f("conv_b")[0]), "lng": _chan(f("ln_g")[0]), "lnb": _chan(f("ln_b")[0]),
    }
    shared.update(_consts())
    xs = f("x")
    ms = f("mem")
    maps = []
    for c in range(NCORES):
        m = dict(shared)
        m["x"] = np.ascontiguousarray(xs[c * NB:(c + 1) * NB])
        m["mem"] = np.ascontiguousarray(ms[c * NB:(c + 1) * NB])
        maps.append(m)
    return maps


def kernel(**inputs):
    if "nc" not in _CACHE:
        _CACHE["nc"] = build_program()
    nc = _CACHE["nc"]
    in_maps = make_in_maps(inputs)
    res = run_bass_kernel_spmd(nc, in_maps, core_ids=list(range(NCORES)))
    out = np.concatenate([np.asarray(r["y"], dtype=np.float32) for r in res.results], axis=0)
    return out
```

```python
from contextlib import ExitStack

import numpy as np
import ml_dtypes

import concourse.bass as bass
import concourse.mybir as mybir
from concourse.bass_utils import run_bass_kernel_spmd

F32 = mybir.dt.float32
BF16 = mybir.dt.bfloat16
AF = mybir.ActivationFunctionType
ALU = mybir.AluOpType

NCORES = 8
NB = 2
S = 2048
D = 1024
KC = 8
DIN = 2568
OFF_U, OFF_G, OFF_Q, OFF_K, OFF_V, OFF_F = 0, 512, 1024, 1536, 2048, 2560
DFF = 2816
FJ = 22
MEM = 256
NT = S // 128
TG = 256
TPG = TG // 128
TGC = 512
CK = 31
EPS = 1e-6
SBUF_BASE = 16512
SBUF_LIMIT = 229312


class Buf:
    __slots__ = ("name", "writers", "readers")

    def __init__(self, name):
        self.name = name
        self.writers = []
        self.readers = []


class Op:
    __slots__ = ("eng", "fn", "deps", "needs_inc", "ticket", "lane", "lane_count", "is_dma")

    def __init__(self, eng, fn):
        self.eng = eng
        self.fn = fn
        self.deps = []
        self.needs_inc = False
        self.ticket = None
        self.lane = None
        self.lane_count = None
        self.is_dma = False


ENGS = ("pe", "act", "dve", "pool", "sp")


class Sched:
    def __init__(self, nc, stack):
        self.nc = nc
        self.stack = stack
        self.streams = {e: [] for e in ENGS}
        self.lane_counts = {}
        self.lane_last = {}
        self.lane_sems = {}
        self.eng_sems = {}
        self.pending = {e: [] for e in ENGS}
        self.nbuf = 0

    def buf(self, name=None):
        self.nbuf += 1
        return Buf(name or "b%d" % self.nbuf)

    def bufs(self, n, name="b"):
        return [self.buf("%s%d" % (name, i)) for i in range(n)]

    def _add_dep(self, op, dep):
        if dep is op:
            return
        if dep.is_dma:
            op.deps.append((dep, self.lane_counts[dep.lane]))
        else:
            if dep.eng == "pe" and op.eng == "pe" and not op.is_dma:
                return
            op.deps.append((dep, None))

    def barrier(self, exclude_lanes=()):
        lasts = []
        for e in ENGS:
            for o in reversed(self.streams[e]):
                if not o.is_dma:
                    lasts.append(o)
                    break
        lasts.extend(o for ln, o in self.lane_last.items() if ln not in exclude_lanes)
        for e in ENGS:
            self.pending[e] = list(lasts)

    def op(self, eng, fn, reads=(), writes=(), lane=None):
        o = Op(eng, fn)
        if lane is not None:
            o.is_dma = True
            o.lane = lane
        if self.pending[eng]:
            for d in self.pending[eng]:
                if d.is_dma or d.eng != eng:
                    self._add_dep(o, d)
            self.pending[eng] = []
        for b in reads:
            for w in b.writers:
                self._add_dep(o, w)
        for b in writes:
            same_gen = (
                o.is_dma and b.writers and not b.readers
                and all(w.is_dma and w.lane == lane for w in b.writers)
            )
            if same_gen:
                b.writers.append(o)
                continue
            for r in b.readers:
                self._add_dep(o, r)
            for w in b.writers:
                self._add_dep(o, w)
            b.writers = [o]
            b.readers = []
        for b in reads:
            if o not in b.readers:
                b.readers.append(o)
        if o.is_dma:
            self.lane_counts[lane] = self.lane_counts.get(lane, 0) + 1
            o.lane_count = self.lane_counts[lane]
            self.lane_last[lane] = o
        self.streams[eng].append(o)
        return o

    def finalize(self):
        nc = self.nc
        for e in ENGS:
            for o in self.streams[e]:
                for d, _ in o.deps:
                    if not d.is_dma:
                        d.needs_inc = True
        for e in ENGS:
            t = 0
            for o in self.streams[e]:
                if o.needs_inc and not o.is_dma:
                    t += 1
                    o.ticket = t
        for e in ENGS:
            self.eng_sems[e] = self.stack.enter_context(nc.semaphore("s_" + e))
        for ln in self.lane_counts:
            self.lane_sems[ln] = self.stack.enter_context(nc.semaphore("l_%s" % (ln,)))

    def replay(self, e, engine, final_lanes=()):
        known = {}
        for o in self.streams[e]:
            need = {}
            for d, lc in o.deps:
                if d.is_dma:
                    key = ("lane", d.lane)
                    val = 16 * lc
                else:
                    key = ("eng", d.eng)
                    val = d.ticket
                if known.get(key, 0) >= val:
                    continue
                if need.get(key, 0) < val:
                    need[key] = val
            for key, val in need.items():
                sem = self.lane_sems[key[1]] if key[0] == "lane" else self.eng_sems[key[1]]
                engine.wait_ge(sem, val)
                known[key] = val
            ins = o.fn(engine)
            if o.is_dma:
                ins.then_inc(self.lane_sems[o.lane], 16)
            elif o.needs_inc:
                ins.then_inc(self.eng_sems[e], 1)
        for ln in final_lanes:
            engine.wait_ge(self.lane_sems[ln], 16 * self.lane_counts[ln])

    def run(self, final_lanes=()):
        nc = self.nc
        self.finalize()
        print("ops", {e: len(v) for e, v in self.streams.items()},
              "tickets", {e: max([o.ticket or 0 for o in v] + [0]) for e, v in self.streams.items()},
              "lanes", len(self.lane_counts), max(self.lane_counts.values()))
        with nc.Block() as block:
            @block.tensor
            def _(eng):
                self.replay("pe", eng)

            @block.scalar
            def _(eng):
                self.replay("act", eng)

            @block.vector
            def _(eng):
                self.replay("dve", eng)

            @block.gpsimd
            def _(eng):
                self.replay("pool", eng)

            @block.sync
            def _(eng):
                self.replay("sp", eng, final_lanes=final_lanes)


class Arena:
    def __init__(self, nc, base=0):
        self.nc = nc
        self.off = base
        self.n = 0

    def alloc(self, name, shape, dt):
        nbytes = int(np.prod(shape[1:])) * (4 if dt == F32 else 2)
        nbytes = (nbytes + 31) // 32 * 32
        self.n += 1
        t = self.nc.alloc_sbuf_tensor_at("%s_%d_%d" % (name, self.off, self.n), list(shape), dt, offset=self.off)
        self.off += nbytes
        assert self.off <= SBUF_LIMIT, (name, self.off)
        return t


def build_program(debug=False, stop=None, nb_a=NB, ng_a=S // TG, nt_b=NB * NT, ng_c=NB * S // TGC):
    nc = bass.Bass("TRN2", target_bir_lowering=False)

    def din(name, shape, dt=F32):
        return nc.dram_tensor(name, list(shape), dt, kind="ExternalInput").ap()

    x = din("x", [NB, S, D])
    mem = din("mem", [NB, MEM, D])
    w_in = din("w_in", [D, DIN])
    w_out = din("w_out", [D, D])
    w_mq = din("w_mq", [D, D])
    w_mkv = din("w_mkv", [D, 2 * D])
    w_mo = din("w_mo", [D, D])
    w_gu = din("w_gu", [D, 2 * DFF])
    w_down = din("w_down", [DFF, D])
    g_mix = din("g_mix", [D])
    g_x = din("g_x", [D])
    g_mem = din("g_mem", [D])
    g_ffn = din("g_ffn", [D])
    g_final = din("g_final", [D])
    b_f = din("b_f", [8])
    convw = din("convw", [128, 4, CK])
    convb = din("convb", [128, 4])
    lng = din("lng", [128, 4])
    lnb = din("lnb", [128, 4])
    c_identb = din("c_identb", [128, 128], BF16)
    c_identf = din("c_identf", [128, 128])
    c_utri = din("c_utri", [128, 128])
    c_onesf = din("c_onesf", [128, 128])
    c_mask = din("c_mask", [128, 128], BF16)
    c_onesb = din("c_onesb", [1, 8 * S], BF16)
    y = nc.dram_tensor("y", [NB, S, D], F32, kind="ExternalOutput").ap()
    skind = "ExternalOutput" if debug else "Internal"
    x1s = nc.dram_tensor("x1s", [NB, S, D], F32, kind=skind).ap()
    x2s = nc.dram_tensor("x2s", [NB, S, D], F32, kind=skind).ap()

    with ExitStack() as st:
        S_ = Sched(nc, st)
        op = S_.op
        nb = S_.buf

        def ps(name, shape, dt=F32):
            return st.enter_context(nc.psum_tensor(name, list(shape), dt))

        AR = Arena(nc, SBUF_BASE)
        identb = AR.alloc("identb", [128, 128], BF16)
        identf = AR.alloc("identf", [128, 128], F32)
        utri = AR.alloc("utri", [128, 128], F32)
        onesf = AR.alloc("onesf", [128, 128], F32)
        cmask = AR.alloc("cmask", [128, 128], BF16)
        inv512 = AR.alloc("inv512", [128, 128], BF16)
        oneb = AR.alloc("oneb", [128, 8], BF16)
        epst = AR.alloc("epst", [128, 1], F32)
        onec = AR.alloc("onec", [128, 1], F32)
        ss = [AR.alloc("ss", [128, 1], F32) for _ in range(4)]
        lnt = [AR.alloc("lnt", [128, 1], F32) for _ in range(4)]
        rstd = [AR.alloc("rstd", [128, 1], F32) for _ in range(4)]
        junk = AR.alloc("junk", [128, D], BF16)
        xin = [AR.alloc("xin", [128, D], F32) for _ in range(2)]
        xres = [AR.alloc("xres", [128, D], F32) for _ in range(2)]
        hb = [AR.alloc("hb", [128, D], BF16) for _ in range(2)]
        PBASE = AR.off

        B_const = nb("const")
        B_ss = S_.bufs(4, "ss")
        B_lnt = S_.bufs(4, "lnt")
        B_rstd = S_.bufs(4, "rstd")
        B_junk = nb("junk")
        B_xin = S_.bufs(2, "xin")
        B_xres = S_.bufs(2, "xres")
        B_hb = S_.bufs(2, "hb")

        p_tr = ps("p_tr", [128, 8, 128], BF16)
        p_z = [ps("p_z%d" % i, [128, 512]) for i in range(2)]
        p_cv = ps("p_cv", [128, 2, 256])
        p_st = ps("p_st", [128, 2, 256])
        p_s = [ps("p_s%d" % i, [128, 2, 256]) for i in range(2)]
        p_o = ps("p_o", [128, 512])
        B_ptr = nb("ptr")
        B_pz = [[nb("pz%d" % i)] for i in range(2)]
        _bpcv, _bpst = nb("pcv"), nb("pst")
        B_pcv = [_bpcv, _bpcv]
        B_pst = [_bpst, _bpst]
        B_ps = [nb("ps%d" % i) for i in range(2)]
        B_pob = nb("po")
        B_po = [B_pob, _bpcv]

        for (t, src, nm) in ((identb, c_identb, "c0"), (identf, c_identf, "c1"), (utri, c_utri, "c2"),
                             (onesf, c_onesf, "c3"), (cmask, c_mask, "c4")):
            op("sp", lambda e, t=t, src=src: e.dma_start(out=t[:], in_=src[:, :]), writes=[B_const], lane=nm)

        def consts_init(e):
            e.memset(inv512[:], 1.0 / 512)
            e.memset(oneb[:], 1.0)
            e.memset(epst[:], EPS)
            return e.memset(onec[:], 1.0)
        B_c2 = nb("c2")
        op("pool", consts_init, writes=[B_c2])

        def sp(n=2):
            for _ in range(n):
                yield

        def norm_tile_g(slot, gt, B_g, src_t, B_src, dst_hT, B_dst, spaced=False):
            op("act", lambda e: e.activation(out=junk[:], in_=src_t[:], func=AF.Square, scale=1.0 / 32,
                                             accum_out=ss[slot][:]),
               reads=[B_src], writes=[B_junk, B_ss[slot]])
            yield
            if spaced:
                yield from sp(1)
            op("act", lambda e: e.activation(out=lnt[slot][:], in_=ss[slot][:], func=AF.Ln, bias=epst[:, 0:1]),
               reads=[B_ss[slot], B_c2], writes=[B_lnt[slot]])
            op("act", lambda e: e.activation(out=rstd[slot][:], in_=lnt[slot][:], func=AF.Exp, scale=-0.5),
               reads=[B_lnt[slot]], writes=[B_rstd[slot]])
            yield
            if spaced:
                yield from sp(1)
            op("dve", lambda e: e.scalar_tensor_tensor(out=hb[slot][:], in0=src_t[:], scalar=rstd[slot][:, 0:1],
                                                       in1=gt[:], op0=ALU.mult, op1=ALU.mult),
               reads=[B_src, B_rstd[slot], B_g], writes=[B_hb[slot]])
            yield
            if spaced:
                yield from sp(2)

            def tr(e):
                ins = None
                for c in range(KC):
                    ins = e.transpose(out=p_tr[:, c, :], in_=hb[slot][:, c * 128:(c + 1) * 128], identity=identb[:])
                return ins
            op("pe", tr, reads=[B_hb[slot], B_const], writes=[B_ptr])
            op("dve", lambda e: e.tensor_copy(out=dst_hT, in_=p_tr[:]), reads=[B_ptr], writes=[B_dst])
            yield

        def norm_tile(*a):
            for _ in norm_tile_g(*a):
                pass

        def wload(dst, src2d, kchunks, B_w, lane, extra_writes=(), after=()):
            sv = src2d.rearrange("(c p) n -> p c n", p=128)
            for c in range(kchunks):
                op("pool", lambda e, c=c: e.dma_start(out=dst[:, c, :], in_=sv[:, c, :]),
                   reads=list(after), writes=[B_w] + list(extra_writes), lane=lane)

        def bload(dst, vec, B_g, lane):
            op("sp", lambda e: e.dma_start(out=dst[:], in_=vec.partition_broadcast(128)), writes=[B_g], lane=lane)

        A = Arena(nc, PBASE)
        win = A.alloc("win", [128, KC, DIN], BF16)
        A_WOUT_OFF = A.off
        wout = A.alloc("wout", [128, KC, D], BF16)
        KT = A.alloc("KT", [65, 8, S], BF16)
        A_VT_OFF = A.off
        Vt = A.alloc("Vt", [128, NT, 8, 65], BF16)
        Dg = A.alloc("Dg", [128, 4, CK, 128], BF16)
        A_DG_END = A.off
        gmix = A.alloc("gmix", [128, D], F32)
        cw = A.alloc("cw", [128, 4, CK], F32)
        cb = A.alloc("cb", [128, 4], F32)
        lg_ = A.alloc("lg_", [128, 4], F32)
        lb_ = A.alloc("lb_", [128, 4], F32)
        nlb = A.alloc("nlb", [128, 4], F32)
        bft = A.alloc("bft", [128, 8], F32)
        hT = A.alloc("hT", [128, KC, TG], BF16)
        QT = A.alloc("QT", [65, 8, TG], BF16)
        a_t = A.alloc("a_t", [128, 4, 30 + TG], BF16)
        sg = [A.alloc("sg", [128, TG], F32) for _ in range(4)]
        ybf = A.alloc("ybf", [128, 4, TG], BF16)
        ysq = A.alloc("ysq", [128, 4, TG], BF16)
        msb = A.alloc("msb", [128, TG], F32)
        var_t = A.alloc("var_t", [128, TG], F32)
        rs_t = A.alloc("rs_t", [128, TG], F32)
        tmpd = [A.alloc("tmpd", [128, TG], F32) for _ in range(4)]
        mixT = A.alloc("mixT", [128, KC, TG], BF16)
        NPT = 6
        PT = [A.alloc("PT", [128, TG], BF16) for _ in range(NPT)]
        att = [A.alloc("att", [128, 512], BF16) for _ in range(TPG)]
        cum = A.alloc("cum", [128, NT, 8], F32)
        lgsum = A.alloc("lgsum", [128, 8], F32)
        f1 = A.alloc("f1", [128, 8], F32)
        f2 = A.alloc("f2", [128, 8], F32)
        lgt = A.alloc("lgt", [128, 8], F32)
        rT = A.alloc("rT", [8, TG], BF16)
        rec = [A.alloc("rec", [128, 2], F32) for _ in range(3)]
        print("phase A sbuf end", A.off)

        B_win, B_wout, B_gmix, B_cst = nb("win"), nb("wout"), nb("gmix"), nb("cst")
        B_Dg = nb("Dg")
        B_KT = S_.bufs(8, "KT")
        B_KT1 = nb("KT1")
        B_V = S_.bufs(NT, "V")
        B_V1 = nb("V1")
        B_hT = S_.bufs(TPG, "hT")
        B_QTq = S_.bufs(8, "QTq")
        B_QTr = nb("QTr")
        B_nlb = nb("nlb")
        B_a = S_.bufs(4, "a")
        B_sg = S_.bufs(4, "sg")
        B_ybf = S_.bufs(4, "ybf")
        B_ysq = S_.bufs(4, "ysq")
        B_msb, B_var, B_rs = nb("msb"), nb("var"), nb("rs")
        B_tmpd = S_.bufs(4, "tmpd")
        B_mixc = S_.bufs(4, "mixc")
        B_mixa = S_.bufs(TPG, "mixa")
        B_PT = S_.bufs(NPT, "PT")
        B_att = [[nb("att%d_%d" % (t, h)) for h in range(8)] for t in range(TPG)]
        B_cum = S_.bufs(NT, "cum")
        B_lgsum, B_f1, B_f2, B_lgt, B_rT = nb("lgsum"), nb("f1"), nb("f2"), nb("lgt"), nb("rT")
        B_rec = S_.bufs(3, "rec")
        B_x1 = [[nb("x1_%d_%d" % (b, t)) for t in range(NT)] for b in range(NB)]
        B_x2 = [[nb("x2_%d_%d" % (b, t)) for t in range(NT)] for b in range(NB)]

        bload(gmix, g_mix, B_gmix, "gmix")
        for (t, src, nm) in ((cw, convw, "k0"), (cb, convb, "k1"), (lg_, lng, "k2"), (lb_, lnb, "k3")):
            op("sp", lambda e, t=t, src=src: e.dma_start(out=t[:], in_=src), writes=[B_cst], lane=nm)
        op("sp", lambda e: e.dma_start(out=bft[:], in_=b_f.partition_broadcast(128)), writes=[B_cst], lane="k4")
        wload(win, w_in, KC, B_win, "win", after=[B_const, B_gmix, B_cst])
        wload(wout, w_out, KC, B_wout, "wout", after=[B_win])

        def mk_dg(e):
            ins = None
            for c in range(4):
                for k in range(CK):
                    ins = e.tensor_scalar(out=Dg[:, c, k, :], in0=identf[:], scalar1=cw[:, c, k:k + 1],
                                          scalar2=None, op0=ALU.mult)
            return ins

        op("pool", lambda e: e.tensor_scalar(out=nlb[:], in0=lb_[:], scalar1=-1.0, scalar2=None, op0=ALU.mult),
           reads=[B_cst], writes=[B_nlb])

        op("sp", lambda e: e.dma_start(out=KT[64:65, :, :], in_=c_onesb.rearrange("o (h s) -> o h s", h=8)),
           writes=[B_KT1], lane="kt1")
        op("pool", lambda e: e.memset(Vt[:, :, :, 64:65], 1.0), writes=[B_V1])

        zslot = [0]

        def next_z():
            i = zslot[0] % 2
            zslot[0] += 1
            return p_z[i][:, 0:256], B_pz[i][0]

        evac_rr = [0]

        def evac(out_ap, in_ap, reads, writes):
            evac_rr[0] += 1
            if evac_rr[0] % 2:
                op("act", lambda e: e.copy(out=out_ap, in_=in_ap), reads=reads, writes=writes)
            else:
                op("dve", lambda e: e.tensor_copy(out=out_ap, in_=in_ap), reads=reads, writes=writes)

        def proj_fm(col0, m, out_ps, B_out):
            def f(e):
                ins = None
                for kc in range(KC):
                    ins = e.matmul(out_ps, lhsT=win[:, kc, col0:col0 + m], rhs=hT[:, kc, :],
                                   start=(kc == 0), stop=(kc == KC - 1))
                return ins
            op("pe", f, reads=[B_win] + B_hT, writes=[B_out])

        tile_ctr = [0]
        pt_ctr = [0]
        po_ctr = [0]
        s_ctr = [0]
        z_ctr = [0]
        p_cv_flat = p_cv[:, :, :].rearrange("p a b -> p (a b)")
        zb = [(p_z[0], B_pz[0][0]), (p_z[1], B_pz[1][0]), (p_cv_flat, _bpcv), (p_o, B_pob)]

        def zbank():
            i = z_ctr[0] % 4
            z_ctr[0] += 1
            return zb[i]

        def run(gen):
            for _ in gen:
                pass

        def interleave(gens, weights):
            live = [(g_, w) for g_, w in zip(gens, weights) if g_ is not None]
            while live:
                nxt = []
                for g_, w in live:
                    alive = True
                    for _ in range(w):
                        try:
                            next(g_)
                        except StopIteration:
                            alive = False
                            break
                    if alive:
                        nxt.append((g_, w))
                live = nxt

        def prep(b, g):
            tok0 = g * TG
            for t in range(TPG):
                slot = tile_ctr[0] % 2
                tile_ctr[0] += 1
                tk = tok0 + t * 128
                op("sp", lambda e, slot=slot, tk=tk, b=b: e.dma_start(out=xin[slot][:], in_=x[b, tk:tk + 128, :]),
                   writes=[B_xin[slot]], lane="xin%d" % slot)
                yield from norm_tile_g(slot, gmix, B_gmix, xin[slot], B_xin[slot],
                                       hT[:, :, t * 128:(t + 1) * 128], B_hT[t], spaced=True)

        def fchain(b, g):
            tiles = [g * TPG + t for t in range(TPG)]
            for t in range(TPG):
                j = tiles[t]
                pf = p_st[:, 0, 0:8]

                def ff(e, t=t, pf=pf):
                    ins = None
                    for kc in range(KC):
                        ins = e.matmul(pf, lhsT=hT[:, kc, t * 128:(t + 1) * 128],
                                       rhs=win[:, kc, OFF_F:OFF_F + 8], start=(kc == 0), stop=(kc == KC - 1))
                    return ins
                op("pe", ff, reads=[B_win, B_hT[t]], writes=[B_pst[0]])
                op("dve", lambda e, pf=pf: e.tensor_tensor(out=f1[:], in0=pf, in1=bft[:], op=ALU.add),
                   reads=[B_pst[0], B_cst], writes=[B_f1])
                yield
                op("act", lambda e: e.activation(out=f2[:], in_=f1[:], func=AF.Exp, scale=-1.0),
                   reads=[B_f1], writes=[B_f2])
                op("act", lambda e: e.activation(out=lgt[:], in_=f2[:], func=AF.Ln, bias=onec[:, 0:1]),
                   reads=[B_f2, B_c2], writes=[B_lgt])
                yield
                pc = p_st[:, 0, 16:24]

                def fc(e, pc=pc):
                    e.matmul(pc, lhsT=utri[:], rhs=lgt[:], start=True, stop=False)
                    return e.matmul(pc, lhsT=onesf[:], rhs=lgsum[:], start=False, stop=True)
                op("pe", fc, reads=[B_lgt, B_lgsum, B_const], writes=[B_pst[0]])
                op("dve", lambda e, j=j, pc=pc: e.tensor_copy(out=cum[:, j, :], in_=pc),
                   reads=[B_pst[0]], writes=[B_cum[j]])
                op("pool", lambda e: e.tensor_tensor(out=lgsum[:], in0=lgsum[:], in1=lgt[:], op=ALU.add),
                   reads=[B_lgsum, B_lgt], writes=[B_lgsum])
                yield
                pr = p_st[0:8, 1, 0:128]
                op("pe", lambda e, j=j, pr=pr: e.transpose(out=pr, in_=cum[:, j, :], identity=identf[:]),
                   reads=[B_cum[j], B_const], writes=[B_pst[1]])
                op("act", lambda e, t=t, pr=pr: e.activation(out=rT[0:8, t * 128:(t + 1) * 128], in_=pr,
                                                            func=AF.Copy, scale=-8.0),
                   reads=[B_pst[1]], writes=[B_rT])
                yield
            for h in range(8):
                op("sp", lambda e, h=h: e.dma_start(out=QT[64:65, h, :], in_=rT[h:h + 1, :]),
                   reads=[B_rT], writes=[B_QTr], lane="qtr")
            yield

        def projmain(b, g):
            tok0 = g * TG
            tiles = [g * TPG + t for t in range(TPG)]
            for hp in range(4):
                for kind in range(2):
                    pz_, Bz = zbank()
                    proj_fm((OFF_Q, OFF_K)[kind] + hp * 128, 128, pz_[:, 0:TG], Bz)
                    for hh in range(2):
                        h = 2 * hp + hh
                        src = pz_[hh * 64:(hh + 1) * 64, 0:TG]
                        if kind == 0:
                            dst, Bd = QT[0:64, h, :], B_QTq[h]
                        else:
                            dst, Bd = KT[0:64, h, tok0:tok0 + TG], B_KT[h]
                        if hh == 0:
                            op("act", lambda e, dst=dst, src=src: e.copy(out=dst, in_=src), reads=[Bz], writes=[Bd])
                        else:
                            op("dve", lambda e, dst=dst, src=src: e.tensor_copy(out=dst, in_=src), reads=[Bz], writes=[Bd])
                    yield
            for t in range(TPG):
                j = tiles[t]
                pv, Bv = zbank()

                def fv(e, t=t, pv=pv):
                    ins = None
                    for kc in range(KC):
                        ins = e.matmul(pv[:, :], lhsT=hT[:, kc, t * 128:(t + 1) * 128],
                                       rhs=win[:, kc, OFF_V:OFF_V + 512], start=(kc == 0), stop=(kc == KC - 1))
                    return ins
                op("pe", fv, reads=[B_win, B_hT[t]], writes=[Bv])
                evac(Vt[:, j, :, 0:64], pv[:, :].rearrange("p (h d) -> p h d", h=8), [Bv], [B_V[j]])
                yield
            for c in range(4):
                pu, Bu = zbank()
                proj_fm(OFF_U + c * 128, 128, pu[:, 0:TG], Bu)
                pg, Bg = zbank()
                proj_fm(OFF_G + c * 128, 128, pg[:, 0:TG], Bg)
                sl = c % 2
                op("act", lambda e, pg=pg, sl=sl: e.activation(out=sg[sl][:], in_=pg[:, 0:TG], func=AF.Exp, scale=-1.0),
                   reads=[Bg], writes=[B_sg[sl]])
                op("dve", lambda e, sl=sl: e.tensor_scalar(out=sg[sl][:], in0=sg[sl][:], scalar1=1.0, scalar2=None,
                                                           op0=ALU.add),
                   reads=[B_sg[sl]], writes=[B_sg[sl]])
                op("dve", lambda e, sl=sl: e.reciprocal(out=sg[sl][:], in_=sg[sl][:]),
                   reads=[B_sg[sl]], writes=[B_sg[sl]])
                op("dve", lambda e, pu=pu, sl=sl, c=c: e.tensor_tensor(out=a_t[:, c, 30:30 + TG], in0=pu[:, 0:TG],
                                                                      in1=sg[sl][:], op=ALU.mult),
                   reads=[Bu, B_sg[sl]], writes=[B_a[c]])
                yield

        def convchain(b, g):
            NPART = 4
            for c in range(4):
                pcv, Bcv = p_z[0][:, (c % 2) * TG:(c % 2) * TG + TG], B_pz[0][0]
                for part in range(NPART):
                    k0, k1 = part * 8, min(CK, part * 8 + 8)

                    def fcv(e, c=c, pcv=pcv, k0=k0, k1=k1):
                        ins = None
                        for k in range(k0, k1):
                            ins = e.matmul(pcv, lhsT=Dg[:, c, k, :], rhs=a_t[:, c, k:k + TG],
                                           start=(k == 0), stop=(k == CK - 1))
                        return ins
                    op("pe", fcv, reads=[B_Dg, B_a[c]], writes=[Bcv])
                    yield
                yield from sp(2)
                op("act", lambda e, c=c, pcv=pcv: e.activation(out=ybf[:, c, :], in_=pcv, func=AF.Identity,
                                                              bias=cb[:, c:c + 1]),
                   reads=[Bcv, B_cst], writes=[B_ybf[c]])
                yield from sp(2)
                op("pool", lambda e, c=c: e.tensor_tensor(out=ysq[:, c, :], in0=ybf[:, c, :], in1=ybf[:, c, :], op=ALU.mult),
                   reads=[B_ybf[c]], writes=[B_ysq[c]])
                yield
            op("pool", lambda e: e.tensor_copy(out=a_t[:, :, 0:30], in_=a_t[:, :, TG:TG + 30]),
               reads=B_a, writes=B_a)
            yield from sp(2)

            def fst(e):
                ins = None
                for c in range(4):
                    ins = e.matmul(p_st[:, 0, :], lhsT=inv512[:], rhs=ybf[:, c, :], start=(c == 0), stop=(c == 3))
                for c in range(4):
                    ins = e.matmul(p_st[:, 1, :], lhsT=inv512[:], rhs=ysq[:, c, :], start=(c == 0), stop=(c == 3))
                return ins
            op("pe", fst, reads=B_ybf + B_ysq + [B_c2], writes=B_pst)
            yield from sp(3)
            op("act", lambda e: e.copy(out=msb[:], in_=p_st[:, 0, :]), reads=[B_pst[0]], writes=[B_msb])
            yield from sp(2)
            op("dve", lambda e: e.tensor_tensor(out=var_t[:], in0=msb[:], in1=msb[:], op=ALU.mult),
               reads=[B_msb], writes=[B_var])
            op("dve", lambda e: e.tensor_tensor(out=var_t[:], in0=p_st[:, 1, :], in1=var_t[:], op=ALU.subtract),
               reads=[B_pst[1], B_var], writes=[B_var])
            yield from sp(2)
            op("act", lambda e: e.activation(out=rs_t[:], in_=var_t[:], func=AF.Ln, bias=epst[:, 0:1]),
               reads=[B_var, B_c2], writes=[B_rs])
            op("act", lambda e: e.activation(out=rs_t[:], in_=rs_t[:], func=AF.Exp, scale=-0.5),
               reads=[B_rs], writes=[B_rs])
            yield from sp(2)
            for c in range(4):
                op("dve", lambda e, c=c: e.tensor_tensor(out=tmpd[c][:], in0=ybf[:, c, :], in1=msb[:], op=ALU.subtract),
                   reads=[B_ybf[c], B_msb], writes=[B_tmpd[c]])
                op("dve", lambda e, c=c: e.scalar_tensor_tensor(out=tmpd[c][:], in0=tmpd[c][:], scalar=lg_[:, c:c + 1],
                                                               in1=rs_t[:], op0=ALU.mult, op1=ALU.mult),
                   reads=[B_tmpd[c], B_rs, B_cst], writes=[B_tmpd[c]])
                yield
            yield from sp(1)
            for c in range(4):
                op("act", lambda e, c=c: e.activation(out=sg[c][:], in_=tmpd[c][:], func=AF.Exp,
                                                      bias=nlb[:, c:c + 1], scale=-1.0),
                   reads=[B_tmpd[c], B_nlb], writes=[B_sg[c]])
                yield
            yield from sp(1)
            for c in range(4):
                op("dve", lambda e, c=c: e.tensor_scalar(out=sg[c][:], in0=sg[c][:], scalar1=1.0, scalar2=None, op0=ALU.add),
                   reads=[B_sg[c]], writes=[B_sg[c]])
                op("dve", lambda e, c=c: e.reciprocal(out=sg[c][:], in_=sg[c][:]),
                   reads=[B_sg[c]], writes=[B_sg[c]])
                op("dve", lambda e, c=c: e.scalar_tensor_tensor(out=mixT[:, c, :], in0=tmpd[c][:], scalar=lb_[:, c:c + 1],
                                                               in1=sg[c][:], op0=ALU.add, op1=ALU.mult),
                   reads=[B_tmpd[c], B_sg[c], B_cst], writes=[B_mixc[c]])
                yield

        def attn(b, g):
            tiles = [g * TPG + t for t in range(TPG)]
            i_last = tiles[-1]
            steps = [(h, j) for h in range(8) for j in range(i_last + 1)]
            info = {}

            def emit_qk(s):
                h, j = steps[s]
                t_lo = max(0, j - tiles[0])
                q0 = t_lo * 128
                si = s_ctr[0] % 3
                s_ctr[0] += 1
                sbank = (p_s[0][:, 0, :], p_s[1][:, 0, :], p_z[1][:, 0:TG])[si]
                psv = sbank[:, q0:TG]
                diag = j >= tiles[0]

                def fqk(e, h=h, j=j, q0=q0, psv=psv, diag=diag, sbank=sbank):
                    ins = e.matmul(psv, lhsT=KT[0:65, h, j * 128:(j + 1) * 128], rhs=QT[0:65, h, q0:TG],
                                   start=True, stop=not diag)
                    if diag:
                        ins = e.matmul(sbank[:, q0:q0 + 128], lhsT=identb[:], rhs=cmask[:],
                                       start=False, stop=True)
                    return ins
                op("pe", fqk, reads=[B_KT[h], B_KT1, B_QTq[h], B_QTr, B_const], writes=[B_s3[si]])
                info[s] = (t_lo, q0, si, psv)

            B_s3 = [B_ps[0], B_ps[1], B_pz[1][0]]
            emit_qk(0)
            if len(steps) > 1:
                emit_qk(1)
            pos = 0
            po_v = None
            for s in range(len(steps)):
                h, j = steps[s]
                if j == 0:
                    pos = po_ctr[0] % 2
                    po_ctr[0] += 1
                    po_v = (p_o[:, 0:130], p_cv[:, 0, 0:130])[pos].rearrange("p (t d) -> p t d", t=TPG)
                if s + 2 < len(steps):
                    emit_qk(s + 2)
                t_lo, q0, si, psv = info.pop(s)
                pi = pt_ctr[0] % NPT
                pt_ctr[0] += 1
                op("act", lambda e, h=h, j=j, q0=q0, psv=psv, pi=pi: e.activation(
                    out=PT[pi][:, q0:TG], in_=psv, func=AF.Exp, bias=cum[:, j, h:h + 1], scale=0.125),
                   reads=[B_s3[si], B_cum[j]], writes=[B_PT[pi]])

                def fpv(e, h=h, j=j, t_lo=t_lo, pi=pi, po_v=po_v):
                    ins = None
                    for t in range(t_lo, TPG):
                        ins = e.matmul(po_v[:, t, :], lhsT=PT[pi][:, t * 128:(t + 1) * 128],
                                       rhs=Vt[:, j, h, :], start=(j == 0 and t == 0), stop=(j == tiles[t]),
                                       skip_group_check=True)
                    return ins
                op("pe", fpv, reads=[B_PT[pi], B_V[j], B_V1], writes=[B_po[pos]])
                if j == i_last:
                    op("dve", lambda e, pos=pos, po_v=po_v: e.reciprocal(out=rec[pos][:, 0:TPG], in_=po_v[:, :, 64]),
                       reads=[B_po[pos]], writes=[B_rec[pos]])
                    for t in range(TPG):
                        op("dve", lambda e, t=t, h=h, pos=pos, po_v=po_v: e.tensor_scalar(
                            out=att[t][:, h * 64:(h + 1) * 64], in0=po_v[:, t, 0:64], scalar1=rec[pos][:, t:t + 1],
                            scalar2=None, op0=ALU.mult),
                           reads=[B_po[pos], B_rec[pos]], writes=[B_att[t][h]])
                yield

        def outproj1(b, g):
            for t in range(TPG):
                j_ = g * TPG + t
                tk_ = g * TG + t * 128
                rsl_ = j_ % 2
                op("sp", lambda e, rsl_=rsl_, tk_=tk_, b=b: e.dma_start(out=xres[rsl_][:], in_=x[b, tk_:tk_ + 128, :]),
                   writes=[B_xres[rsl_]], lane="xres%d" % rsl_)
            for t in range(TPG):
                def tra(e, t=t):
                    ins = None
                    for c in range(4):
                        ins = e.transpose(out=p_tr[:, c, :], in_=att[t][:, c * 128:(c + 1) * 128], identity=identb[:])
                    return ins
                op("pe", tra, reads=B_att[t] + [B_const], writes=[B_ptr])
                op("dve", lambda e, t=t: e.tensor_copy(out=mixT[:, 4:8, t * 128:(t + 1) * 128], in_=p_tr[:, 0:4, :]),
                   reads=[B_ptr], writes=[B_mixa[t]])
                yield

        def outproj2(b, g):
            tok0 = g * TG
            tiles = [g * TPG + t for t in range(TPG)]
            pw2 = [p_s[0][:, :, :].rearrange("p a b -> p (a b)"), p_s[1][:, :, :].rearrange("p a b -> p (a b)")]
            for t in range(TPG):
                j = tiles[t]
                tk = tok0 + t * 128
                rsl = j % 2
                for half in range(2):
                    def fo(e, t=t, half=half):
                        ins = None
                        for kc in range(KC):
                            ins = e.matmul(pw2[half], lhsT=mixT[:, kc, t * 128:(t + 1) * 128],
                                           rhs=wout[:, kc, half * 512:(half + 1) * 512],
                                           start=(kc == 0), stop=(kc == KC - 1))
                        return ins
                    op("pe", fo, reads=B_mixc + [B_mixa[t], B_wout], writes=[B_ps[half]])
                    op("dve", lambda e, rsl=rsl, half=half: e.tensor_tensor(
                        out=xres[rsl][:, half * 512:(half + 1) * 512], in0=pw2[half],
                        in1=xres[rsl][:, half * 512:(half + 1) * 512], op=ALU.add),
                       reads=[B_ps[half], B_xres[rsl]], writes=[B_xres[rsl]])
                    yield
                op("sp", lambda e, rsl=rsl, tk=tk, b=b: e.dma_start(out=x1s[b, tk:tk + 128, :], in_=xres[rsl][:]),
                   reads=[B_xres[rsl]], writes=[B_x1[b][j]], lane="xres%d" % rsl)
                yield

        Bm = Arena(nc, PBASE)
        wmq = Bm.alloc("wmq", [128, KC, D], BF16)
        wmo = Bm.alloc("wmo", [128, KC, D], BF16)
        GU_OFF = SBUF_LIMIT - KC * 2 * DFF * 2
        Wk = Arena(nc, GU_OFF)
        wmkv = Wk.alloc("wmkv", [128, KC, 2 * D], BF16)
        B_wmq, B_wmo, B_wmkv, B_wgu = nb("wmq"), nb("wmo"), nb("wmkv"), nb("wgu")
        assert Bm.off <= A_WOUT_OFF and GU_OFF >= A_VT_OFF and GU_OFF + KC * 2 * D * 2 <= A_DG_END

        groups = [(b, g) for b in range(nb_a) for g in range(ng_a)]
        run(prep(*groups[0]))
        op("dve", mk_dg, reads=[B_const, B_cst], writes=[B_Dg])
        for idx, (b, g) in enumerate(groups):
            if g == 0:
                op("pool", lambda e: e.memset(a_t[:, :, 0:30], 0.0), writes=B_a)
                op("pool", lambda e: e.memset(lgsum[:], 0.0), writes=[B_lgsum])
            prev_out = outproj2(*groups[idx - 1]) if idx > 0 else None
            interleave([fchain(b, g), projmain(b, g), prev_out], [1, 1, 1])
            last = idx == len(groups) - 1 and stop is None
            if last:
                wload(wmq, w_mq, KC, B_wmq, "wmq", extra_writes=[B_win])
                wload(wmo, w_mo, KC, B_wmo, "wmo", extra_writes=[B_win])
            nxt = prep(*groups[idx + 1]) if idx + 1 < len(groups) else None

            def attn_tr(b=b, g=g):
                yield from attn(b, g)
                yield from outproj1(b, g)
            interleave([attn_tr(), convchain(b, g), nxt], [1, 1, 1])
            if last:
                wload(wmkv, w_mkv, KC, B_wmkv, "wmkv", extra_writes=B_V + [B_V1, B_Dg])
        run(outproj2(*groups[-1]))

        if stop == "A":
            S_.run(final_lanes=list(S_.lane_counts))
            return nc
        S_.barrier(exclude_lanes=("wmq", "wmo", "wmkv"))
        mKT = [Bm.alloc("mKT", [128, 8, MEM], BF16) for _ in range(NB)]
        mV = [Bm.alloc("mV", [128, 2, D], BF16) for _ in range(NB)]
        gxt = Bm.alloc("gxt", [128, D], F32)
        gmt = Bm.alloc("gmt", [128, D], F32)
        NSB = 3
        hTb = [Bm.alloc("hTb", [128, KC, 128], BF16) for _ in range(NSB)]
        qTb = [Bm.alloc("qTb", [128, 8, 128], BF16) for _ in range(NSB)]
        PTb = [Bm.alloc("PTb", [128, 8, 128], BF16) for _ in range(NSB)]
        onb = [Bm.alloc("onb", [128, D], BF16) for _ in range(NSB)]
        oTb = [Bm.alloc("oTb", [128, KC, 128], BF16) for _ in range(NSB)]
        rdn = [Bm.alloc("rdn", [128, 4], F32) for _ in range(NSB)]
        xin.append(Bm.alloc("xin2", [128, D], F32))
        hb.append(Bm.alloc("hb2", [128, D], BF16))
        B_xin.append(nb("xin2"))
        B_hb.append(nb("hb2"))
        print("phase B sbuf end", Bm.off)
        assert Bm.off <= GU_OFF
        Wg = Arena(nc, GU_OFF)
        wgu = Wg.alloc("wgu", [128, KC, 2 * DFF], BF16)

        B_gx, B_gm = nb("gx"), nb("gm")
        B_mKT = [[nb("mKT%d_%d" % (b, c)) for c in range(16)] for b in range(NB)]
        B_mV = [[nb("mV%d_%d" % (b, c)) for c in range(4)] for b in range(NB)]
        B_hTb, B_qTb, B_PTb, B_onb, B_oTb, B_rdn = (S_.bufs(NSB, n) for n in ("hTb", "qTb", "PTb", "onb", "oTb", "rdn"))
        B_qTb = [[nb("qTb%d_%d" % (s, c)) for c in range(2)] for s in range(NSB)]
        B_PTb = [[nb("PTb%d_%d" % (s, c)) for c in range(2)] for s in range(NSB)]
        B_onbh = [[nb("onb%d_%d" % (s, c)) for c in range(4)] for s in range(NSB)]

        if stop is not None:
            wload(wmkv, w_mkv, KC, B_wmkv, "wmkv")
            wload(wmq, w_mq, KC, B_wmq, "wmq", after=[B_wmkv])
            wload(wmo, w_mo, KC, B_wmo, "wmo", after=[B_wmq])
        bload(gmt, g_mem, B_gm, "gmt")
        bload(gxt, g_x, B_gx, "gxt")

        p_q = p_z[0]
        p_l = p_z[1]
        B_pq, B_pl = nb("pq"), nb("pl")
        B_pob = S_.bufs(2, "pob")
        B_pwo = S_.bufs(2, "pwo")
        B_pden = nb("pden")
        po_b = [p_cv, p_st]
        pwo_b = p_s

        def mtile(idx):
            b, mt = divmod(idx, 2)
            slot = idx % NSB
            op("sp", lambda e, slot=slot, mt=mt, b=b: e.dma_start(out=xin[slot][:], in_=mem[b, mt * 128:(mt + 1) * 128, :]),
               writes=[B_xin[slot]], lane="xin%d" % slot)
            yield from norm_tile_g(slot, gmt, B_gm, xin[slot], B_xin[slot], hTb[slot][:, :, :], B_hTb[slot])
            for c4 in range(2):
                def fk(e, c4=c4, slot=slot):
                    ins = None
                    for cc in range(4):
                        c = c4 * 4 + cc
                        for kc in range(KC):
                            ins = e.matmul(p_q[:, cc * 128:(cc + 1) * 128], lhsT=wmkv[:, kc, c * 128:(c + 1) * 128],
                                           rhs=hTb[slot][:, kc, :], start=(kc == 0), stop=(kc == KC - 1))
                    return ins
                op("pe", fk, reads=[B_wmkv, B_hTb[slot]], writes=[B_pq])
                evac(mKT[b][:, c4 * 4:(c4 + 1) * 4, mt * 128:(mt + 1) * 128],
                     p_q[:, :].rearrange("p (c m) -> p c m", c=4), [B_pq], [B_mKT[b][mt * 2 + c4]])
                yield
            for half in range(2):
                def fmv(e, half=half, slot=slot):
                    ins = None
                    for kc in range(KC):
                        ins = e.matmul(p_l[:, :], lhsT=hTb[slot][:, kc, :],
                                       rhs=wmkv[:, kc, D + half * 512:D + (half + 1) * 512],
                                       start=(kc == 0), stop=(kc == KC - 1))
                    return ins
                op("pe", fmv, reads=[B_wmkv, B_hTb[slot]], writes=[B_pl])
                evac(mV[b][:, mt, half * 512:(half + 1) * 512], p_l[:, :], [B_pl], [B_mV[b][mt * 2 + half]])
                yield

        def pipeline(gens, lag):
            gens = list(gens)
            live = []
            tick = 0
            nxt_i = 0
            last_start = -lag
            while live or nxt_i < len(gens):
                if nxt_i < len(gens) and tick - last_start >= lag and len(live) < NSB:
                    live.append(gens[nxt_i])
                    nxt_i += 1
                    last_start = tick
                keep = []
                for g_ in live:
                    try:
                        next(g_)
                        keep.append(g_)
                    except StopIteration:
                        pass
                live = keep
                tick += 1

        pipeline([mtile(i) for i in range(2 * NB)], lag=3)
        wload(wgu, w_gu, KC, B_wgu, "wgu", extra_writes=[B_wmkv])

        def btile(ti):
            b, j = divmod(ti, NT)
            slot = ti % NSB
            tk = j * 128
            op("sp", lambda e, slot=slot, tk=tk, b=b: e.dma_start(out=xin[slot][:], in_=x1s[b, tk:tk + 128, :]),
               reads=[B_x1[b][j]], writes=[B_xin[slot]], lane="xin%d" % slot)
            yield from norm_tile_g(slot, gxt, B_gx, xin[slot], B_xin[slot], hTb[slot][:, :, :], B_hTb[slot])
            for c4 in range(2):
                def fq(e, c4=c4, slot=slot):
                    ins = None
                    for cc in range(4):
                        c = c4 * 4 + cc
                        for kc in range(KC):
                            ins = e.matmul(p_q[:, cc * 128:(cc + 1) * 128], lhsT=wmq[:, kc, c * 128:(c + 1) * 128],
                                           rhs=hTb[slot][:, kc, :], start=(kc == 0), stop=(kc == KC - 1))
                    return ins
                op("pe", fq, reads=[B_wmq, B_hTb[slot]], writes=[B_pq])
                evac(qTb[slot][:, c4 * 4:(c4 + 1) * 4, :], p_q[:, :].rearrange("p (c m) -> p c m", c=4),
                     [B_pq], [B_qTb[slot][c4]])
                yield
            for hp in range(2):
                def fl(e, hp=hp, slot=slot, b=b):
                    ins = None
                    for hh in range(2):
                        hd = hp * 2 + hh
                        for mt in range(2):
                            o4 = hh * 2 + mt
                            for dc in range(2):
                                ins = e.matmul(p_l[:, o4 * 128:(o4 + 1) * 128],
                                               lhsT=mKT[b][:, hd * 2 + dc, mt * 128:(mt + 1) * 128],
                                               rhs=qTb[slot][:, hd * 2 + dc, :], start=(dc == 0), stop=(dc == 1))
                    return ins
                op("pe", fl, reads=B_mKT[b] + [B_qTb[slot][hp]], writes=[B_pl])
                op("act", lambda e, hp=hp, slot=slot: e.activation(
                    out=PTb[slot][:, hp * 4:(hp + 1) * 4, :], in_=p_l[:, :].rearrange("p (c m) -> p c m", c=4),
                    func=AF.Exp, scale=1.0 / 16),
                   reads=[B_pl], writes=[B_PTb[slot][hp]])
                yield
            for hd in range(4):
                hp, hh = divmod(hd, 2)
                pob = po_b[hd // 2]

                def fpv2(e, hd=hd, hp=hp, hh=hh, pob=pob, slot=slot, b=b):
                    ins = None
                    for mt in range(2):
                        ins = e.matmul(pob[:, hh, :], lhsT=PTb[slot][:, hp * 4 + hh * 2 + mt, :],
                                       rhs=mV[b][:, mt, hd * 256:(hd + 1) * 256], start=(mt == 0), stop=(mt == 1))
                    for mt in range(2):
                        ins = e.matmul(p_o[:, hd:hd + 1], lhsT=PTb[slot][:, hp * 4 + hh * 2 + mt, :],
                                       rhs=oneb[:, 0:1], start=(mt == 0), stop=(mt == 1))
                    return ins
                op("pe", fpv2, reads=[B_PTb[slot][hp], B_c2] + B_mV[b], writes=[B_pob[hd // 2], B_pden])
                if hd % 2:
                    yield
            op("dve", lambda e, slot=slot: e.reciprocal(out=rdn[slot][:], in_=p_o[:, 0:4]),
               reads=[B_pden], writes=[B_rdn[slot]])
            for hd in range(4):
                pob = po_b[hd // 2]
                eng = "dve"
                if eng == "dve":
                    op("dve", lambda e, hd=hd, pob=pob, slot=slot: e.tensor_scalar(
                        out=onb[slot][:, hd * 256:(hd + 1) * 256], in0=pob[:, hd % 2, :], scalar1=rdn[slot][:, hd:hd + 1],
                        scalar2=None, op0=ALU.mult),
                       reads=[B_pob[hd // 2], B_rdn[slot]], writes=[B_onbh[slot][hd]])
                else:
                    op("act", lambda e, hd=hd, pob=pob, slot=slot: e.activation(
                        out=onb[slot][:, hd * 256:(hd + 1) * 256], in_=pob[:, hd % 2, :], func=AF.Identity,
                        scale=rdn[slot][:, hd:hd + 1]),
                       reads=[B_pob[hd // 2], B_rdn[slot]], writes=[B_onbh[slot][hd]])
            yield

            def tro(e, slot=slot):
                ins = None
                for c in range(KC):
                    ins = e.transpose(out=p_tr[:, c, :], in_=onb[slot][:, c * 128:(c + 1) * 128], identity=identb[:])
                return ins
            op("pe", tro, reads=B_onbh[slot] + [B_const], writes=[B_ptr])
            op("dve", lambda e, slot=slot: e.tensor_copy(out=oTb[slot][:, :, :], in_=p_tr[:]),
               reads=[B_ptr], writes=[B_oTb[slot]])
            yield
            for half in range(2):
                pw = pwo_b[half]

                def fwo(e, half=half, slot=slot, pw=pw):
                    ins = None
                    for kc in range(KC):
                        ins = e.matmul(pw[:, :, :].rearrange("p a b -> p (a b)"), lhsT=oTb[slot][:, kc, :],
                                       rhs=wmo[:, kc, half * 512:(half + 1) * 512], start=(kc == 0), stop=(kc == KC - 1))
                    return ins
                op("pe", fwo, reads=[B_oTb[slot], B_wmo], writes=[B_pwo[half]])
                op("dve", lambda e, half=half, slot=slot, pw=pw: e.tensor_tensor(
                    out=xin[slot][:, half * 512:(half + 1) * 512], in0=pw[:, :, :].rearrange("p a b -> p (a b)"),
                    in1=xin[slot][:, half * 512:(half + 1) * 512], op=ALU.add),
                   reads=[B_pwo[half], B_xin[slot]], writes=[B_xin[slot]])
                yield
            op("sp", lambda e, slot=slot, tk=tk, b=b: e.dma_start(out=x2s[b, tk:tk + 128, :], in_=xin[slot][:]),
               reads=[B_xin[slot]], writes=[B_x2[b][j]], lane="xin%d" % slot)
            yield

        pipeline([btile(ti) for ti in range(nt_b)], lag=5)

        if stop == "B":
            S_.run(final_lanes=list(S_.lane_counts))
            return nc
        S_.barrier()
        C = Arena(nc, PBASE)
        wdn = C.alloc("wdn", [128, FJ, D], BF16)
        gft = C.alloc("gft", [128, D], F32)
        gfin = C.alloc("gfin", [128, D], F32)
        hTc = [C.alloc("hTc", [128, KC, TGC], BF16) for _ in range(2)]
        actT = C.alloc("actT", [128, FJ, TGC], BF16)
        slt = [C.alloc("slt", [128, TGC], F32) for _ in range(2)]
        print("phase C sbuf end", C.off)
        assert C.off <= GU_OFF
        B_wdn, B_gf, B_gfin = nb("wdn"), nb("gf"), nb("gfin")
        B_hTc = [S_.bufs(TGC // 128, "hTc%d_" % i) for i in range(2)]
        B_actT = S_.bufs(FJ, "actT")
        B_slt = S_.bufs(2, "slt")
        wload(wdn, w_down, FJ, B_wdn, "wdn")
        bload(gft, g_ffn, B_gf, "gft")
        bload(gfin, g_final, B_gfin, "gfin")
        gu_banks = [(p_z[0][:, :], p_z[1][:, :]),
                    (p_cv[:, :, :].rearrange("p a b -> p (a b)"), p_st[:, :, :].rearrange("p a b -> p (a b)"))]
        B_gub = [(nb("gb0"), nb("ub0")), (nb("gb1"), nb("ub1"))]
        pdn = [p_s[0][:, :, :].rearrange("p a b -> p (a b)"), p_s[1][:, :, :].rearrange("p a b -> p (a b)")]
        B_pdn = S_.bufs(2, "pdn")
        NTC = TGC // 128

        def cprep(gi):
            hs = gi % 2
            for t in range(NTC):
                ti = gi * NTC + t
                b, j = divmod(ti, NT)
                slot = ti % 2
                tk = j * 128
                op("sp", lambda e, slot=slot, tk=tk, b=b: e.dma_start(out=xin[slot][:], in_=x2s[b, tk:tk + 128, :]),
                   reads=[B_x2[b][j]], writes=[B_xin[slot]], lane="xin%d" % slot)
                yield from norm_tile_g(slot, gft, B_gf, xin[slot], B_xin[slot],
                                       hTc[hs][:, :, t * 128:(t + 1) * 128], B_hTc[hs][t], spaced=True)

        def cmain(gi):
            hs = gi % 2
            for jf in range(FJ):
                gb = jf % 2
                pg, pu = gu_banks[gb]
                Bg, Bu = B_gub[gb]

                def fgu(e, jf=jf, pg=pg, pu=pu, hs=hs):
                    ins = None
                    for kc in range(KC):
                        ins = e.matmul(pg, lhsT=wgu[:, kc, jf * 128:(jf + 1) * 128], rhs=hTc[hs][:, kc, :],
                                       start=(kc == 0), stop=(kc == KC - 1))
                    for kc in range(KC):
                        ins = e.matmul(pu, lhsT=wgu[:, kc, DFF + jf * 128:DFF + (jf + 1) * 128], rhs=hTc[hs][:, kc, :],
                                       start=(kc == 0), stop=(kc == KC - 1))
                    return ins
                op("pe", fgu, reads=[B_wgu] + B_hTc[hs], writes=[Bg, Bu])
                sl = jf % 2
                op("act", lambda e, pg=pg, sl=sl: e.activation(out=slt[sl][:], in_=pg, func=AF.Silu),
                   reads=[Bg], writes=[B_slt[sl]])
                op("dve", lambda e, pu=pu, sl=sl, jf=jf: e.tensor_tensor(out=actT[:, jf, :], in0=pu, in1=slt[sl][:],
                                                                        op=ALU.mult),
                   reads=[Bu, B_slt[sl]], writes=[B_actT[jf]])
                yield
            for t in range(NTC):
                ti = gi * NTC + t
                b, j = divmod(ti, NT)
                rsl = ti % 2
                tk = j * 128
                op("sp", lambda e, rsl=rsl, tk=tk, b=b: e.dma_start(out=xres[rsl][:], in_=x2s[b, tk:tk + 128, :]),
                   reads=[B_x2[b][j]], writes=[B_xres[rsl]], lane="xres%d" % rsl)
                for half in range(2):
                    def fd(e, t=t, half=half):
                        ins = None
                        for jf in range(FJ):
                            ins = e.matmul(pdn[half], lhsT=actT[:, jf, t * 128:(t + 1) * 128],
                                           rhs=wdn[:, jf, half * 512:(half + 1) * 512],
                                           start=(jf == 0), stop=(jf == FJ - 1))
                        return ins
                    op("pe", fd, reads=B_actT + [B_wdn], writes=[B_pdn[half]])
                    op("dve", lambda e, rsl=rsl, half=half: e.tensor_tensor(
                        out=xres[rsl][:, half * 512:(half + 1) * 512], in0=pdn[half],
                        in1=xres[rsl][:, half * 512:(half + 1) * 512], op=ALU.add),
                       reads=[B_pdn[half], B_xres[rsl]], writes=[B_xres[rsl]])
                    yield
                sl = 2 + rsl
                op("act", lambda e, rsl=rsl, sl=sl: e.activation(out=junk[:], in_=xres[rsl][:], func=AF.Square,
                                                                 scale=1.0 / 32, accum_out=ss[sl][:]),
                   reads=[B_xres[rsl]], writes=[B_junk, B_ss[sl]])
                op("act", lambda e, sl=sl: e.activation(out=lnt[sl][:], in_=ss[sl][:], func=AF.Ln, bias=epst[:, 0:1]),
                   reads=[B_ss[sl], B_c2], writes=[B_lnt[sl]])
                op("act", lambda e, sl=sl: e.activation(out=rstd[sl][:], in_=lnt[sl][:], func=AF.Exp, scale=-0.5),
                   reads=[B_lnt[sl]], writes=[B_rstd[sl]])
                yield
                op("dve", lambda e, rsl=rsl, sl=sl: e.scalar_tensor_tensor(
                    out=xres[rsl][:], in0=xres[rsl][:], scalar=rstd[sl][:, 0:1], in1=gfin[:],
                    op0=ALU.mult, op1=ALU.mult),
                   reads=[B_xres[rsl], B_rstd[sl], B_gfin], writes=[B_xres[rsl]])
                op("sp", lambda e, rsl=rsl, tk=tk, b=b: e.dma_start(out=y[b, tk:tk + 128, :], in_=xres[rsl][:]),
                   reads=[B_xres[rsl]], writes=[nb()], lane="xres%d" % rsl)
                yield

        if ng_c:
            run(cprep(0))
        for gi in range(ng_c):
            nxt = cprep(gi + 1) if gi + 1 < ng_c else None
            interleave([cmain(gi), nxt], [1, 1])

        S_.run(final_lanes=["xres0", "xres1", "xin0", "xin1"])
    return nc


_CACHE = {}


def _consts():
    bf = ml_dtypes.bfloat16
    ident = np.eye(128, dtype=np.float32)
    utri = np.triu(np.ones((128, 128), dtype=np.float32))
    mask = np.where(np.arange(128)[:, None] <= np.arange(128)[None, :], 0.0, -30000.0).astype(np.float32)
    return {
        "c_identb": ident.astype(bf),
        "c_identf": ident,
        "c_utri": utri,
        "c_onesf": np.ones((128, 128), dtype=np.float32),
        "c_mask": mask.astype(bf),
        "c_onesb": np.ones((1, 8 * S), dtype=bf),
    }


def _chan(v):
    return np.ascontiguousarray(np.asarray(v, dtype=np.float32).reshape(4, 128).T)


def make_in_maps(inputs):
    f = lambda k: np.ascontiguousarray(np.asarray(inputs[k], dtype=np.float32))
    shared = {
        "w_in": f("w_in")[0], "w_out": f("w_out")[0], "w_mq": f("w_mq")[0], "w_mkv": f("w_mkv")[0],
        "w_mo": f("w_mo")[0], "w_gu": f("w_gu")[0], "w_down": f("w_down")[0],
        "g_mix": f("g_mix")[0], "g_x": f("g_x")[0], "g_mem": f("g_mem"), "g_ffn": f("g_ffn")[0],
        "g_final": f("g_final"), "b_f": f("b_f")[0],
        "convw": np.ascontiguousarray(f("conv_w")[0].T.reshape(4, 128, CK).transpose(1, 0, 2)),
        "convb": _chan(f("conv_b")[0]), "lng": _chan(f("ln_g")[0]), "lnb": _chan(f("ln_b")[0]),
    }
    shared.update(_consts())
    xs = f("x")
    ms = f("mem")
    maps = []
    for c in range(NCORES):
        m = dict(shared)
        m["x"] = np.ascontiguousarray(xs[c * NB:(c + 1) * NB])
        m["mem"] = np.ascontiguousarray(ms[c * NB:(c + 1) * NB])
        maps.append(m)
    return maps


def kernel(**inputs):
    if "nc" not in _CACHE:
        _CACHE["nc"] = build_program()
    nc = _CACHE["nc"]
    in_maps = make_in_maps(inputs)
    res = run_bass_kernel_spmd(nc, in_maps, core_ids=list(range(NCORES)))
    out = np.concatenate([np.asarray(r["y"], dtype=np.float32) for r in res.results], axis=0)
    return out
```
